# Optimizing a Trainium2 kernel written in Bass

```python
import jax
import jax.numpy as jnp
from jax import lax
import numpy as np

D_MODEL = 2048
BATCH = 1
SEQ = 16384
DEPTH = 4
DEC_BATCH = 32
DEC_SEQ = 16
PAST_LEN = 4096

CHUNK = 64
CONV_DIM = D_MODEL // 2
CONV_WIDTH = 31
HEAD_DIM = 64
N_Q_HEADS = (D_MODEL // 2) // HEAD_DIM
N_KV_HEADS = 4
GROUP = N_Q_HEADS // N_KV_HEADS
ATTN_DIM = N_Q_HEADS * HEAD_DIM
KV_DIM = N_KV_HEADS * HEAD_DIM
WINDOW = 128
WIN_CHUNKS = WINDOW // CHUNK
ROPE_DIM = HEAD_DIM // 4
ROPE_THETA = 500000.0
D_FF = ((8 * D_MODEL + 3 * 256 - 1) // (3 * 256)) * 256
EPS = 1e-6
SPLIT_SIZES = (CONV_DIM, CONV_DIM, ATTN_DIM, KV_DIM, KV_DIM, D_MODEL, D_MODEL)
IN_DIM = sum(SPLIT_SIZES)

kernel_name = 'chunk_stream_conv_swa_hybrid'


def _rms_norm(x, g):
    xf = x.astype(jnp.float32)
    y = xf * lax.rsqrt(jnp.mean(xf * xf, axis=-1, keepdims=True) + EPS)
    return (y * g.astype(jnp.float32)).astype(x.dtype)


def _layer_norm(x, g, b):
    xf = x.astype(jnp.float32)
    xc = xf - jnp.mean(xf, axis=-1, keepdims=True)
    var = jnp.mean(xc * xc, axis=-1, keepdims=True)
    y = xc * lax.rsqrt(var + EPS) * g.astype(jnp.float32) + b.astype(jnp.float32)
    return y.astype(x.dtype)


def _partial_rope(x, pos):
    half = ROPE_DIM // 2
    inv_freq = ROPE_THETA ** (-jnp.arange(half, dtype=jnp.float32) * 2.0 / ROPE_DIM)
    ang = pos[:, None] * inv_freq[None, :]
    cos = jnp.cos(ang)[None, :, None, :]
    sin = jnp.sin(ang)[None, :, None, :]
    xf = x.astype(jnp.float32)
    x1 = xf[..., :half]
    x2 = xf[..., half:ROPE_DIM]
    y = jnp.concatenate([x1 * cos - x2 * sin, x2 * cos + x1 * sin, xf[..., ROPE_DIM:]], axis=-1)
    return y.astype(x.dtype)


def _sink_attention(qb, kb, vb, mask, sink):
    s = jnp.einsum('bnqkgd,bnskd->bnkgqs', qb, kb, preferred_element_type=jnp.float32) * (HEAD_DIM ** -0.5)
    s = jnp.where(mask[None, :, None, None], s, -jnp.inf)
    sk = sink.astype(jnp.float32)[None, None, :, :, None, None]
    m = jnp.maximum(jnp.max(s, axis=-1, keepdims=True), sk)
    e = jnp.exp(s - m)
    p = e / (jnp.sum(e, axis=-1, keepdims=True) + jnp.exp(sk - m))
    o = jnp.einsum('bnkgqs,bnskd->bnqkgd', p, vb.astype(jnp.float32))
    return o.astype(qb.dtype)


def _attn_prompt(q, k, v, sink):
    B, S = q.shape[0], q.shape[1]
    nc = S // CHUNK
    pad = WIN_CHUNKS * CHUNK
    kp = jnp.pad(k, ((0, 0), (pad, 0), (0, 0), (0, 0))).reshape(B, nc + WIN_CHUNKS, CHUNK, N_KV_HEADS, HEAD_DIM)
    vp = jnp.pad(v, ((0, 0), (pad, 0), (0, 0), (0, 0))).reshape(B, nc + WIN_CHUNKS, CHUNK, N_KV_HEADS, HEAD_DIM)
    kb = jnp.concatenate([kp[:, j:j + nc] for j in range(WIN_CHUNKS + 1)], axis=2)
    vb = jnp.concatenate([vp[:, j:j + nc] for j in range(WIN_CHUNKS + 1)], axis=2)
    n_keys = (WIN_CHUNKS + 1) * CHUNK
    key_chunk = jnp.arange(nc)[:, None] - WIN_CHUNKS + (jnp.arange(n_keys) // CHUNK)[None, :]
    mask = jnp.broadcast_to((key_chunk >= 0)[:, None, :], (nc, CHUNK, n_keys))
    qb = q.reshape(B, nc, CHUNK, N_KV_HEADS, GROUP, HEAD_DIM)
    o = _sink_attention(qb, kb, vb, mask, sink)
    return o.reshape(B, S, ATTN_DIM)


def _attn_sample(q, k, v, cache_k, cache_v, sink):
    B, T = q.shape[0], q.shape[1]
    kf = jnp.concatenate([cache_k.astype(k.dtype), k], axis=1)
    vf = jnp.concatenate([cache_v.astype(v.dtype), v], axis=1)
    qb = q.reshape(B, 1, T, N_KV_HEADS, GROUP, HEAD_DIM)
    mask = jnp.ones((1, T, WINDOW + T), dtype=bool)
    o = _sink_attention(qb, kf[:, None], vf[:, None], mask, sink)
    return o.reshape(B, T, ATTN_DIM), kf[:, -WINDOW:], vf[:, -WINDOW:]


def _depthwise_causal_conv(u, hist, w_dw, b_dw):
    up = jnp.concatenate([hist.astype(u.dtype), u], axis=1)
    y = lax.conv_general_dilated(up, w_dw[:, None, :].astype(u.dtype), window_strides=(1,), padding='VALID',
                                 dimension_numbers=('NWC', 'WIO', 'NWC'), feature_group_count=CONV_DIM)
    return y + b_dw.astype(u.dtype), up[:, -(CONV_WIDTH - 1):]


def _layer(x, pos, conv_hist, cache_k, cache_v, norm_mix_g, w_in, w_dw, b_dw, conv_ln_g, conv_ln_b,
           w_conv_out, q_norm_g, k_norm_g, sinks, w_attn_out, w_out, norm_ffn_g, w_gate_up, w_down):
    B, T = x.shape[0], x.shape[1]
    h = _rms_norm(x, norm_mix_g)
    z = h @ w_in
    a_lin, a_gate, q, k, v, g_conv, g_attn = jnp.split(z, np.cumsum(SPLIT_SIZES)[:-1].tolist(), axis=-1)
    u = a_lin * jax.nn.sigmoid(a_gate)
    if conv_hist is None:
        conv_hist = jnp.zeros((B, CONV_WIDTH - 1, CONV_DIM), u.dtype)
    c, new_hist = _depthwise_causal_conv(u, conv_hist, w_dw, b_dw)
    c = jax.nn.silu(_layer_norm(c, conv_ln_g, conv_ln_b))
    conv_out = c @ w_conv_out
    q = _partial_rope(_rms_norm(q.reshape(B, T, N_Q_HEADS, HEAD_DIM), q_norm_g), pos)
    k = _partial_rope(_rms_norm(k.reshape(B, T, N_KV_HEADS, HEAD_DIM), k_norm_g), pos)
    v = v.reshape(B, T, N_KV_HEADS, HEAD_DIM)
    sink = sinks.reshape(N_KV_HEADS, GROUP)
    if cache_k is None:
        o = _attn_prompt(q, k, v, sink)
        new_k = k[:, -WINDOW:]
        new_v = v[:, -WINDOW:]
    else:
        o, new_k, new_v = _attn_sample(q, k, v, cache_k, cache_v, sink)
    attn_out = o @ w_attn_out
    mix = jax.nn.sigmoid(g_conv) * conv_out + jax.nn.sigmoid(g_attn) * attn_out
    x = x + mix @ w_out
    hf = _rms_norm(x, norm_ffn_g)
    gate, up = jnp.split(hf @ w_gate_up, 2, axis=-1)
    x = x + (jax.nn.silu(gate) * up) @ w_down
    return x, new_hist, new_k, new_v


def _trunk(x, pos, state_conv, cache_k, cache_v, weights):
    hists, ks, vs = [], [], []
    for l in range(DEPTH):
        lw = [w[l] for w in weights]
        x, nh, nk, nv = _layer(x, pos,
                               None if state_conv is None else state_conv[l],
                               None if cache_k is None else cache_k[l],
                               None if cache_v is None else cache_v[l], *lw)
        hists.append(nh)
        ks.append(nk)
        vs.append(nv)
    return x, jnp.stack(hists), jnp.stack(ks), jnp.stack(vs)


def setup_inputs(seed: int = 0) -> dict:
    key = jax.random.key(seed)
    ks = jax.random.split(key, 20)
    f32 = jnp.float32
    nrm = lambda k, shape, scale: jax.random.normal(k, shape, f32) * scale
    return {
        'x_prompt': nrm(ks[0], (BATCH, SEQ, D_MODEL), 1.0),
        'x_sample': nrm(ks[1], (DEC_BATCH, DEC_SEQ, D_MODEL), 1.0),
        'state_conv': nrm(ks[2], (DEPTH, DEC_BATCH, CONV_WIDTH - 1, CONV_DIM), 0.5),
        'cache_k': nrm(ks[3], (DEPTH, DEC_BATCH, WINDOW, N_KV_HEADS, HEAD_DIM), 1.0),
        'cache_v': nrm(ks[4], (DEPTH, DEC_BATCH, WINDOW, N_KV_HEADS, HEAD_DIM), 1.0),
        'norm_mix_g': 1.0 + nrm(ks[5], (DEPTH, D_MODEL), 0.05),
        'w_in': nrm(ks[6], (DEPTH, D_MODEL, IN_DIM), D_MODEL ** -0.5),
        'w_dw': nrm(ks[7], (DEPTH, CONV_WIDTH, CONV_DIM), CONV_WIDTH ** -0.5),
        'b_dw': nrm(ks[8], (DEPTH, CONV_DIM), 0.02),
        'conv_ln_g': 1.0 + nrm(ks[9], (DEPTH, CONV_DIM), 0.05),
        'conv_ln_b': nrm(ks[10], (DEPTH, CONV_DIM), 0.02),
        'w_conv_out': nrm(ks[11], (DEPTH, CONV_DIM, D_MODEL), CONV_DIM ** -0.5),
        'q_norm_g': 1.0 + nrm(ks[12], (DEPTH, HEAD_DIM), 0.05),
        'k_norm_g': 1.0 + nrm(ks[13], (DEPTH, HEAD_DIM), 0.05),
        'sinks': nrm(ks[14], (DEPTH, N_Q_HEADS), 0.5),
        'w_attn_out': nrm(ks[15], (DEPTH, ATTN_DIM, D_MODEL), ATTN_DIM ** -0.5),
        'w_out': nrm(ks[16], (DEPTH, D_MODEL, D_MODEL), D_MODEL ** -0.5),
        'norm_ffn_g': 1.0 + nrm(ks[17], (DEPTH, D_MODEL), 0.05),
        'w_gate_up': nrm(ks[18], (DEPTH, D_MODEL, 2 * D_FF), D_MODEL ** -0.5),
        'w_down': nrm(ks[19], (DEPTH, D_FF, D_MODEL), D_FF ** -0.5),
    }


def reference(x_prompt, x_sample, state_conv, cache_k, cache_v, norm_mix_g, w_in, w_dw, b_dw, conv_ln_g,
              conv_ln_b, w_conv_out, q_norm_g, k_norm_g, sinks, w_attn_out, w_out, norm_ffn_g, w_gate_up, w_down):
    weights = (norm_mix_g, w_in, w_dw, b_dw, conv_ln_g, conv_ln_b, w_conv_out, q_norm_g, k_norm_g, sinks,
               w_attn_out, w_out, norm_ffn_g, w_gate_up, w_down)
    pos_p = jnp.arange(x_prompt.shape[1], dtype=jnp.float32)
    pos_s = jnp.arange(x_sample.shape[1], dtype=jnp.float32) + PAST_LEN
    y_prompt, conv_p, k_p, v_p = _trunk(x_prompt, pos_p, None, None, None, weights)
    y_sample, conv_s, k_s, v_s = _trunk(x_sample, pos_s, state_conv, cache_k, cache_v, weights)
    return (y_prompt, y_sample, conv_p, k_p, v_p, conv_s, k_s, v_s)
```

```python
import os
import numpy as np
import concourse.bass as bass
import concourse.mybir as mybir
from concourse.bass_utils import run_bass_kernel_spmd

F32, BF16 = mybir.dt.float32, mybir.dt.bfloat16
AF = mybir.ActivationFunctionType
ALU = mybir.AluOpType
AX = mybir.AxisListType

NCORES = 8
D = 2048; KD = 16; DEPTH = 4; CD = 1024; CCH = 8; DFF = 5632; FCH = 44; IN_DIM = 7680
NT = 512; NTILES = 5; HALO = 512; SEQ = 16384; SEQ_PER = 2048
NS = 64; DEC_B = 32; DEC_T = 16; BPC = 4; PAST = 4096; WIN = 128; CW = 31
O_AL, O_AG, O_Q, O_K, O_V, O_GC, O_GA = 0, 1024, 2048, 3072, 3328, 3584, 5632
EPS = 1e-6
V_G1, V_G2, V_BDW, V_LNG, V_LNB, V_WDW, V_GQ, V_GK, V_GQR, V_GKR, V_SNK = 0, 16, 32, 40, 48, 56, 304, 305, 306, 370, 434
LV = 450
V_CM = DEPTH * LV
NV = V_CM + 4
SAMPLE = True
SB_BASE = 16512
SB_LIMIT = 229376


class Res:
    __slots__ = ("name", "last_w", "readers")

    def __init__(self, name):
        self.name = name; self.last_w = None; self.readers = {}


class SemC:
    __slots__ = ("sem", "count")

    def __init__(self, sem):
        self.sem = sem; self.count = 0


class Op:
    __slots__ = ("eng", "fn", "deps", "signal", "sem", "val", "dma", "ndma", "seq")


class Sched:
    ENGS = ("pe", "act", "dve", "pool", "sp")

    def __init__(self):
        self.q = {e: [] for e in self.ENGS}
        self.esem = {}
        self.seq = 0

    def add(self, eng, fn, reads=(), writes=(), dma=None, ndma=1):
        op = Op(); op.eng = eng; op.fn = fn; op.signal = False; op.dma = dma; op.ndma = ndma
        op.sem = None; op.val = None; self.seq += 1; op.seq = self.seq
        deps = {}

        def add_dep(d):
            if d is None:
                return
            if d.dma is None and dma is None and d.eng == eng and eng == "pe":
                return
            deps[id(d)] = d
        for r in reads:
            add_dep(r.last_w)
        for w in writes:
            add_dep(w.last_w)
            for d in w.readers.values():
                add_dep(d)
        op.deps = list(deps.values())
        for d in op.deps:
            d.signal = True
        key = eng if dma is None else ("dma", id(dma))
        for r in reads:
            r.readers[key] = op
        for w in writes:
            w.last_w = op; w.readers = {}
        if dma is not None:
            dma.count += 16 * ndma
            op.sem = dma.sem; op.val = dma.count
        self.q[eng].append(op)
        return op

    def assign(self):
        for e in self.ENGS:
            c = 0
            for op in self.q[e]:
                if op.dma is None and op.signal:
                    c += 1; op.sem = self.esem[e]; op.val = c

    def emit_one(self, e, eng):
        waited = {}
        for op in self.q[e]:
            need = {}
            for d in op.deps:
                k = id(d.sem)
                if d.val > waited.get(k, 0) and d.val > need.get(k, (None, 0))[1]:
                    need[k] = (d.sem, d.val)
            for k, (s, v) in need.items():
                eng.wait_ge(s, v); waited[k] = v
            ins = op.fn(eng)
            if op.dma is not None:
                if not isinstance(ins, (list, tuple)):
                    ins = [ins]
                assert len(ins) == op.ndma
                for i in ins:
                    i.then_inc(op.sem, 16)
            elif op.signal:
                ins.then_inc(op.sem, 1)


def build_program(ntiles=NTILES, depth=DEPTH, do_mixer=True, do_ffn=True, mix_stop=99, dbg=0, do_last=True, do_sample=True):
    nc = bass.Bass("TRN2", target_bir_lowering=False)

    def din(name, shape):
        return nc.dram_tensor(name, list(shape), F32, kind="ExternalInput").ap()

    def dout(name, shape):
        return nc.dram_tensor(name, list(shape), F32, kind="ExternalOutput").ap()
    xin = din("xin", [NTILES * NT + NS, D])
    ropec = din("ropec", [128, NTILES * NT + NS])
    ropes = din("ropes", [128, NTILES * NT + NS])
    vecs_d = din("vecs", [128, NV])
    consts_d = din("consts", [128, 512])
    w_in = din("w_in", [depth, D, IN_DIM])
    w_co = din("w_conv_out", [depth, CD, D])
    w_ao = din("w_attn_out", [depth, CD, D])
    w_o = din("w_out", [depth, D, D])
    w_gu = din("w_gate_up", [depth, D, 2 * DFF])
    w_dn = din("w_down", [depth, DFF, D])
    st_conv = din("state_conv", [DEPTH, BPC, CW - 1, CD])
    c_k = din("cache_k", [DEPTH, BPC, WIN, 256])
    c_v = din("cache_v", [DEPTH, BPC, WIN, 256])
    y_d = dout("y", [SEQ_PER + NS, D])
    convp_d = dout("convp", [DEPTH, CW - 1, CD])
    kp_d = dout("kp", [DEPTH, WIN, 256])
    vp_d = dout("vp", [DEPTH, WIN, 256])
    convs_d = dout("convs", [DEPTH, BPC, CW - 1, CD])
    ks_d = dout("ks", [DEPTH, BPC, WIN, 256])
    vs_d = dout("vs", [DEPTH, BPC, WIN, 256])
    dbg_d = dout("dbg", [24, 128, 512]) if dbg else None
    dbg_n = [0]
    dbg_names = []

    def wview(w, l):
        return w[l].rearrange("(k p) n -> p k n", p=128)

    cur = [SB_BASE]
    hi = [SB_BASE]
    offs = {}

    def alloc(name, shape, dt, at=None):
        nb = int(np.prod(shape[1:])) * (4 if dt == F32 else 2)
        nb = (nb + 63) // 64 * 64
        if at is None:
            off = cur[0]; cur[0] += nb
        else:
            off = at
        hi[0] = max(hi[0], off + nb)
        offs[name] = off
        assert off + nb <= SB_LIMIT, (name, off, nb)
        return nc.alloc_sbuf_tensor_at(name, list(shape), dt, offset=off)

    xT = alloc("xT", [128, KD, NT], F32)
    hT = alloc("hT", [128, KD, 128 + NT], BF16)
    hhist = [alloc(f"hhist{l}", [128, KD, 128], BF16) for l in range(DEPTH)]
    uhist = [alloc(f"uhist{l}", [128, CCH, 30], F32) for l in range(DEPTH)]
    khist = [alloc(f"khist{l}", [128, 4, 128], BF16) for l in range(DEPTH)]
    NSLOT = 4
    wslot = [alloc(f"wslot{i}", [128, 4096], BF16) for i in range(NSLOT)]
    vecs = alloc("vecs", [128, NV], F32)
    consts = alloc("consts", [128, 512], F32)
    ones_bf = alloc("ones_bf", [128, 128], BF16)
    bones_bf = alloc("bones_bf", [128, 128], BF16)
    cosT = alloc("cosT", [128, NT], F32)
    sinT = alloc("sinT", [128, NT], F32)
    small = alloc("small", [128, 64], F32)
    NSF = 6
    sf = [alloc(f"sf{i}", [128, NT], F32) for i in range(NSF)]
    NSB = 3
    sbf = [alloc(f"sbf{i}", [128, NT], BF16) for i in range(NSB)]
    kf = alloc("kf", [128, 4, 128], F32)
    PH = cur[0]
    uT = alloc("uT", [128, CCH, 30 + NT], F32)
    cT = alloc("cT", [128, CCH, NT], F32)
    cn = alloc("cn", [128, CCH, NT], BF16)
    oT = alloc("oT", [128, CCH, NT], BF16)
    kd = alloc("kd", [128, 4, 128 + NT], BF16)
    Va = alloc("Va", [128, 5, 4, 128], BF16)
    Vb = alloc("Vb", [128, 5, 4, 128], BF16)
    PH_END = cur[0]
    mix = alloc("mix", [128, KD, NT], BF16, at=PH)
    qbd = alloc("qbd", [128, 8, 8, 2, 64], BF16, at=PH + (CCH * (30 + NT) * 4 + 63) // 64 * 64)
    actT = alloc("actT", [128, FCH, NT], BF16, at=PH)
    xstage = [alloc(f"xstage{i}", [128, D], F32, at=PH + i * D * 4) for i in range(2)]
    cur[0] = max(cur[0], PH + FCH * NT * 2, PH + 2 * D * 4)
    ckr = alloc("ckr", [128, 4, 256], F32, at=offs["cn"])
    ckd = [alloc(f"ckd{i}", [128, 4, 2, 64], F32, at=offs["cn"] + 4096 + i * 2048) for i in range(2)]
    ostage = alloc("ostage", [128, 1024], F32, at=offs["oT"])
    ident = consts[:, 0:128]
    Rm = consts[:, 128:256]
    ones_f = consts[:, 256:384]

    banks = [nc.alloc_psum_tensor(f"bank{i}", [128, 512], F32) for i in range(8)]

    S = Sched()
    for e_ in S.ENGS:
        S.esem[e_] = nc.alloc_semaphore(f"es_{e_}")
    R = {}

    def res(n):
        r = R.get(n)
        if r is None:
            r = R[n] = Res(n)
        return r

    def rl(prefix, n):
        return [res(f"{prefix}{i}") for i in range(n)]
    bank_i = [0]

    def bank():
        i = bank_i[0] % 8; bank_i[0] += 1
        return banks[i], res(f"bank{i}")
    sf_i = [0]; sb_i = [0]

    def scr():
        i = sf_i[0] % NSF; sf_i[0] += 1
        return sf[i], res(f"sf{i}")

    def scrb():
        i = sb_i[0] % NSB; sb_i[0] += 1
        return sbf[i], res(f"sbf{i}")

    sems = {}

    def semc(name):
        if name not in sems:
            sems[name] = SemC(nc.alloc_semaphore(name))
        return sems[name]

    def ACT(out, in_, func, reads, writes, scale=1.0, bias=0.0):
        S.add("act", lambda e: e.activation(out=out, in_=in_, func=func, bias=bias, scale=scale), reads, writes)

    def TT(eng, out, in0, in1, op, reads, writes):
        S.add(eng, lambda e: e.tensor_tensor(out=out, in0=in0, in1=in1, op=op), reads, writes)

    def STT(eng, out, in0, scalar, in1, op0, op1, reads, writes):
        S.add(eng, lambda e: e.scalar_tensor_tensor(out=out, in0=in0, scalar=scalar, in1=in1, op0=op0, op1=op1), reads, writes)

    def TS(eng, out, in0, s1, s2, op0, op1, reads, writes):
        if s2 is None:
            S.add(eng, lambda e: e.tensor_scalar(out=out, in0=in0, scalar1=s1, scalar2=None, op0=op0), reads, writes)
        else:
            S.add(eng, lambda e: e.tensor_scalar(out=out, in0=in0, scalar1=s1, scalar2=s2, op0=op0, op1=op1), reads, writes)

    def RECIP(out, in_, reads, writes):
        S.add("dve", lambda e: e.reciprocal(out=out, in_=in_), reads, writes)

    def COPY(eng, out, in_, reads, writes):
        if eng == "act":
            ACT(out, in_, AF.Copy, reads, writes)
        else:
            S.add(eng, lambda e: e.tensor_copy(out=out, in_=in_), reads, writes)

    def MM(out, lhsT, rhs, start, stop, reads, writes):
        S.add("pe", lambda e: e.matmul(out, lhsT, rhs, start=start, stop=stop), reads, writes)

    def TR(out, in_, idn, reads, writes):
        S.add("pe", lambda e: e.transpose(out, in_, idn), reads, writes)

    def DMA(eng, out, in_, sem, reads, writes):
        S.add(eng, lambda e: e.dma_start(out=out, in_=in_), reads, writes, dma=sem)

    def dump(name, ap, reads, t, l):
        if not dbg or (t, l) != (1, 0):
            return
        i = dbg_n[0]; dbg_n[0] += 1; dbg_names.append(name)
        sc, rsc = scr()
        shp = list(ap.shape)
        n = int(np.prod(shp[1:]))
        dst = sc[0:shp[0], 0:n]
        if len(shp) == 3:
            dst = dst.rearrange("p (a b) -> p a b", b=shp[2])
        COPY("act", dst, ap, reads, [rsc])
        DMA("sp", dbg_d[i, 0:shp[0], 0:n], sc[0:shp[0], 0:n], semc("dbg"), [rsc], [res(f"o_dbg{i}")])
        out_res.append(res(f"o_dbg{i}"))

    slot_i = [0]

    def wload(parts):
        i = slot_i[0] % NSLOT; slot_i[0] += 1
        sl = wslot[i]; r = res(f"wslot{i}")
        pp = [(d(sl), s) for d, s in parts]
        S.add("pool", lambda e: [e.dma_start(out=d, in_=s) for d, s in pp], (), [r], dma=semc(f"wslot{i}"), ndma=len(pp))
        return sl, r

    def w3(sl, k, n, off=0):
        return sl[:, off:off + k * n].rearrange("p (k n) -> p k n", n=n)

    out_res = []
    PHR = res("PH")

    def all_phase():
        return ([PHR] + [res(n) for n in ("act", "mix", "qbd", "cn", "oT", "kd", "V", "xstage0", "xstage1")]
                + rl("u", CCH) + rl("c", CCH))
    r_vecs = res("vecs"); r_consts = res("consts")
    DMA("sp", vecs[:, :], vecs_d[:, :], semc("setup"), (), [r_vecs])
    DMA("sp", consts[:, :], consts_d[:, :], semc("setup"), (), [r_consts])
    COPY("act", ones_bf[:, :], consts[:, 256:384], [r_consts], [res("ones_bf")])
    COPY("act", bones_bf[:, :], consts[:, 384:512], [r_consts], [res("bones_bf")])
    r_ones = res("ones_bf"); r_bones = res("bones_bf")
    for l in range(DEPTH):
        S.add("dve", lambda e, l=l: e.memset(hhist[l][:, :, :], 0.0), (), [res(f"hhist{l}")])
        S.add("dve", lambda e, l=l: e.memset(uhist[l][:, :, :], 0.0), (), [res(f"uhist{l}")])
        S.add("dve", lambda e, l=l: e.memset(khist[l][:, :, :], 0.0), (), [res(f"khist{l}")])
    S.add("dve", lambda e: e.memset(actT[:, :, :], 0.0), (), all_phase())
    S.add("dve", lambda e: e.memset(Va[:, :, :, :], 0.0), (), [res("V")])
    S.add("dve", lambda e: e.memset(Vb[:, :, :, :], 0.0), (), [res("V")])
    S.add("dve", lambda e: e.memset(kd[:, :, :], 0.0), (), [res("kd")])
    S.add("dve", lambda e: e.memset(oT[:, :, :], 0.0), (), [res("oT")])
    S.add("dve", lambda e: e.memset(hT[:, :, :], 0.0), (), [res("hT")])
    S.add("dve", lambda e: e.memset(xT[:, :, :], 0.0), (), [res("xT")])
    S.add("dve", lambda e: e.memset(kf[:, :, :], 0.0), (), [res("kf")])
    for i in range(NSF):
        S.add("dve", lambda e, i=i: e.memset(sf[i][:, :], 0.0), (), [res(f"sf{i}")])
    for i in range(NSB):
        S.add("dve", lambda e, i=i: e.memset(sbf[i][:, :], 0.0), (), [res(f"sbf{i}")])


    def rmsnorm(l, gcol, c0, tile0, c1=NT):
        N = c1 - c0
        bk, rb = bank()
        for k in range(KD):
            sq, rs = scrb()
            ACT(sq[:, :N], xT[:, k, c0:c1], AF.Square, [res("xT")], [rs])
            MM(bk[:, :N], ones_bf[:, :], sq[:, :N], k == 0, k == KD - 1, [rs, r_ones], [rb])
        sd, rsd = scr()
        ACT(sd[:, :N], bk[:, :N], AF.Sqrt, [rb, res("small")], [rsd], scale=1.0 / D, bias=small[:, 8:9])
        RECIP(sd[:, :N], sd[:, :N], [rsd], [rsd])
        if tile0:
            TS("dve", sd[:, :N], sd[:, :N], vecs[:, V_CM:V_CM + 1], None, ALU.mult, None, [rsd, r_vecs], [rsd])
        for k in range(KD):
            STT("dve", hT[:, k, 128 + c0:128 + c1], xT[:, k, c0:c1], vecs[:, l * LV + gcol + k: l * LV + gcol + k + 1], sd[:, :N],
                ALU.mult, ALU.mult, [res("xT"), rsd, r_vecs], [res("hT")])

    def layer_setup(l):
        b = l * LV
        r = res("small")
        S.add("dve", lambda e: e.memset(small[:, 8:9], EPS), (), [r])
        S.add("dve", lambda e: e.tensor_reduce(out=small[:, 3:4], in_=vecs[:, b + V_GQR:b + V_GQR + 64], axis=AX.X, op=ALU.max, apply_absolute_value=True), [r_vecs], [r])
        S.add("dve", lambda e: e.tensor_reduce(out=small[:, 4:5], in_=vecs[:, b + V_GKR:b + V_GKR + 64], axis=AX.X, op=ALU.max, apply_absolute_value=True), [r_vecs], [r])
        STT("dve", small[:, 0:1], small[:, 3:4], -8.0, small[:, 4:5], ALU.mult, ALU.mult, [r], [r])
        TT("dve", small[:, 1:2], small[:, 0:1], vecs[:, V_CM + 2:V_CM + 3], ALU.add, [r, r_vecs], [r])
        TT("dve", small[:, 2:3], small[:, 0:1], vecs[:, V_CM + 3:V_CM + 4], ALU.add, [r, r_vecs], [r])
        ACT(small[:, 16:32], vecs[:, b + V_SNK:b + V_SNK + 16], AF.Exp, [r, r_vecs], [r], bias=small[:, 0:1])

    def qk_chain(l, bk, rb, c0, gcolidx, is_k, dst_fn, last_tile, j=None, c1=NT):
        N = c1 - c0
        z, rz = scr()
        ACT(z[:, :N], bk[:, :N], AF.Copy, [rb], [rz])
        sq, rs = scrb()
        ACT(sq[:, :N], bk[:, :N], AF.Square, [rb], [rs])
        b2, rb2 = bank()
        MM(b2[:, :N], bones_bf[:, :], sq[:, :N], True, True, [rs, r_bones], [rb2])
        sd, rsd = scr()
        ACT(sd[:, :N], b2[:, :N], AF.Sqrt, [rb2, res("small")], [rsd], scale=1.0 / 64, bias=small[:, 8:9])
        RECIP(sd[:, :N], sd[:, :N], [rsd], [rsd])
        qn, rqn = scr()
        STT("dve", qn[:, :N], z[:, :N], vecs[:, l * LV + gcolidx:l * LV + gcolidx + 1], sd[:, :N], ALU.mult, ALU.mult,
            [rz, rsd, r_vecs], [rqn])
        b3, rb3 = bank()
        MM(b3[:, :N], Rm, qn[:, :N], True, True, [rqn, r_consts], [rb3])
        t1, rt1 = scr()
        TT("dve", t1[:, :N], qn[:, :N], cosT[:, c0:c1], ALU.mult, [rqn, res("rope")], [rt1])
        t2, rt2 = scr()
        TT("dve", t2[:, :N], b3[:, :N], sinT[:, c0:c1], ALU.mult, [rb3, res("rope")], [rt2])
        dst_fn(t1, t2, rt1, rt2)

    def mixer(l, t):
        tile0 = (t == 0)
        last = (t == NTILES - 1) and do_last
        LU = last and os.environ.get('LU', '1') == '1'
        LK = last and os.environ.get('LK', '1') == '1'
        LV_ = last and os.environ.get('LVV', '1') == '1'
        STOP = mix_stop
        c0 = 128 * l if tile0 else 0
        N = NT - c0
        ch0 = c0 // 64
        b = l * LV
        wi = wview(w_in, l)
        r_hT = res("hT"); r_xT = res("xT")
        COPY("pool", hT[:, :, 0:128], hhist[l][:, :, :], [res(f"hhist{l}")], [r_hT])
        rmsnorm(l, V_G1, c0, tile0)
        dump("h0", hT[:, 0, 128:], [r_hT], t, l)
        r_u = rl("u", CCH)
        for r_ in r_u:
            pass
        S.add("pool", lambda e: e.tensor_copy(out=uT[:, :, 0:30], in_=uhist[l][:, :, :]), [res(f"uhist{l}")], all_phase())
        for m in range(CCH):
            sl, rs = wload([(lambda s: w3(s, KD, 128), wi[:, :, O_AL + m * 128:O_AL + (m + 1) * 128]),
                            (lambda s: w3(s, KD, 128, 2048), wi[:, :, O_AG + m * 128:O_AG + (m + 1) * 128])])
            wa = w3(sl, KD, 128); wg = w3(sl, KD, 128, 2048)
            bA, rA = bank(); bG, rG = bank()
            for k in range(KD):
                MM(bA[:, :N], wa[:, k, :], hT[:, k, 128 + c0:], k == 0, k == KD - 1, [rs, r_hT], [rA])
            for k in range(KD):
                MM(bG[:, :N], wg[:, k, :], hT[:, k, 128 + c0:], k == 0, k == KD - 1, [rs, r_hT], [rG])
            sg, rsg = scr()
            ACT(sg[:, :N], bG[:, :N], AF.Sigmoid, [rG], [rsg])
            TT("dve", uT[:, m, 30 + c0:], bA[:, :N], sg[:, :N], ALU.mult, [rA, rsg], [r_u[m]])
        dump("u0", uT[:, 0, 30:], [res("u0")], t, l)
        S.add("pool", lambda e: e.tensor_copy(out=uhist[l][:, :, :], in_=uT[:, :, NT:NT + 30]), r_u, [res(f"uhist{l}")])
        if LU:
            for hlf in range(2):
                bk, rb = bank()
                for m4 in range(4):
                    m = hlf * 4 + m4
                    TR(bk[0:32, m4 * 128:(m4 + 1) * 128], uT[:, m, NT - 2:NT + 30], ident, [r_u[m], r_consts], [rb])
                COPY("act", ostage[0:32, hlf * 512:(hlf + 1) * 512], bk[0:32, :], [rb], [res("oT")])
            DMA("sp", convp_d[l, :, :], ostage[2:32, :], semc("ost"), [res("oT")], [res("o_convp")])
        r_c = rl("c", CCH)
        for m in range(CCH):
            wcol = b + V_WDW + m * CW
            TS("dve", cT[:, m, c0:], uT[:, m, c0:c0 + N], vecs[:, wcol:wcol + 1], vecs[:, b + V_BDW + m:b + V_BDW + m + 1],
               ALU.mult, ALU.add, [r_u[m], r_vecs], [r_c[m]])
            for j in range(1, CW):
                STT("dve", cT[:, m, c0:], uT[:, m, c0 + j:c0 + j + N], vecs[:, wcol + j:wcol + j + 1], cT[:, m, c0:],
                    ALU.mult, ALU.add, [r_u[m], r_c[m], r_vecs], [r_c[m]])
        dump("c0", cT[:, 0, :], [res("c0")], t, l)
        b1, rb1 = bank(); b2, rb2 = bank()
        for m in range(CCH):
            MM(b1[:, :N], ones_f, cT[:, m, c0:], m == 0, m == CCH - 1, [r_c[m], r_consts], [rb1])
        for m in range(CCH):
            sq, rsq = scr()
            ACT(sq[:, :N], cT[:, m, c0:], AF.Square, [r_c[m]], [rsq])
            MM(b2[:, :N], ones_f, sq[:, :N], m == 0, m == CCH - 1, [rsq, r_consts], [rb2])
        mean, rmean = scr()
        TS("dve", mean[:, :N], b1[:, :N], 1.0 / CD, None, ALU.mult, None, [rb1], [rmean])
        msq, rmsq = scr()
        TT("dve", msq[:, :N], mean[:, :N], mean[:, :N], ALU.mult, [rmean], [rmsq])
        var, rvar = scr()
        STT("dve", var[:, :N], b2[:, :N], 1.0 / CD, msq[:, :N], ALU.mult, ALU.subtract, [rb2, rmsq], [rvar])
        ACT(var[:, :N], var[:, :N], AF.Sqrt, [rvar, res("small")], [rvar], bias=small[:, 8:9])
        RECIP(var[:, :N], var[:, :N], [rvar], [rvar])
        STT("dve", msq[:, :N], mean[:, :N], -1.0, var[:, :N], ALU.mult, ALU.mult, [rmean, rvar], [rmsq])
        r_cn = res("cn")
        held = {id(mean), id(msq), id(var)}
        free_sf = [i for i in range(NSF) if id(sf[i]) not in held]
        for m in range(CCH):
            fi = free_sf[m % len(free_sf)]
            tt_, rtt = sf[fi], res(f"sf{fi}")
            TT("dve", tt_[:, :N], cT[:, m, c0:], var[:, :N], ALU.mult, [r_c[m], rvar], [rtt])
            TT("dve", tt_[:, :N], tt_[:, :N], msq[:, :N], ALU.add, [rtt, rmsq], [rtt])
            ACT(cn[:, m, c0:], tt_[:, :N], AF.Silu, [rtt, r_vecs], [r_cn],
                scale=vecs[:, b + V_LNG + m:b + V_LNG + m + 1], bias=vecs[:, b + V_LNB + m:b + V_LNB + m + 1])
        dump("cn0", cn[:, 0, :], [r_cn], t, l)
        r_qbd = res("qbd")
        S.add("pool", lambda e: e.memset(qbd[:, :, :, :, :], 0.0), (), [r_qbd] + r_c)
        for qp in range(4):
            sl, rs = wload([(lambda s: w3(s, KD, 256), wi[:, :, O_Q + qp * 256:O_Q + (qp + 1) * 256])])
            wq = w3(sl, KD, 256)
            for h2 in range(2):
                qc = qp * 2 + h2
                bk, rb = bank()
                for k in range(KD):
                    MM(bk[:, :N], wq[:, k, h2 * 128:(h2 + 1) * 128], hT[:, k, 128 + c0:], k == 0, k == KD - 1, [rs, r_hT], [rb])

                def dst(t1, t2, rt1, rt2, qc=qc):
                    TT("pool", qbd[0:64, ch0:, qc, 0, :], t1[0:64, :N].rearrange("p (c t) -> p c t", t=64),
                       t2[0:64, :N].rearrange("p (c t) -> p c t", t=64), ALU.add, [rt1, rt2], [r_qbd])
                    TT("pool", qbd[64:128, ch0:, qc, 1, :], t1[64:128, :N].rearrange("p (c t) -> p c t", t=64),
                       t2[64:128, :N].rearrange("p (c t) -> p c t", t=64), ALU.add, [rt1, rt2], [r_qbd])
                qk_chain(l, bk, rb, c0, V_GQ, False, dst, last)
        dump("q0", qbd[:, :, 0, 0, :], [r_qbd], t, l)
        r_kd = res("kd")
        COPY("pool", kd[:, :, 0:128], khist[l][:, :, :], [res(f"khist{l}")], [r_kd])
        wk = wi[:, :, O_K:O_K + 256]
        for jp in range(2):
            parts = []
            for jj in range(2):
                j = jp * 2 + jj
                for dup in range(2):
                    parts.append((lambda s, o=(jj * 2 + dup) * 64: w3(s, KD, 256)[:, :, o:o + 64], wk[:, :, j * 64:(j + 1) * 64]))
            sl, rs = wload(parts)
            wkk = w3(sl, KD, 256)
            for jj in range(2):
                j = jp * 2 + jj
                bk, rb = bank()
                for k in range(KD):
                    MM(bk[:, :N], wkk[:, k, jj * 128:(jj + 1) * 128], hT[:, k, 128 + c0:], k == 0, k == KD - 1, [rs, r_hT], [rb])

                def dstk(t1, t2, rt1, rt2, j=j):
                    TT("pool", kd[:, j, 128 + c0:], t1[:, :N], t2[:, :N], ALU.add, [rt1, rt2], [r_kd])
                    if LK:
                        TT("pool", kf[:, j, :], t1[:, NT - 128:NT], t2[:, NT - 128:NT], ALU.add, [rt1, rt2], [res("kf")])
                qk_chain(l, bk, rb, c0, V_GK, True, dstk, last, j)
        S.add("pool", lambda e: e.tensor_copy(out=khist[l][:, :, :], in_=kd[:, :, NT:NT + 128]), [r_kd], [res(f"khist{l}")])
        if LK:
            for j in range(4):
                bk, rb = bank()
                TR(bk[:, 0:128], kf[:, j, :], ident, [res("kf"), r_consts], [rb])
                COPY("act", ostage[:, j * 64:(j + 1) * 64], bk[:, 0:64], [rb], [res("oT")])
            DMA("sp", kp_d[l, :, :], ostage[:, 0:256], semc("ost"), [res("oT")], [res("o_kp")])
        dump("k0", kd[:, 0, 128:], [r_kd], t, l)
        sl, rs = wload([(lambda s: w3(s, KD, 256), wi[:, :, O_V:O_V + 256])])
        wv = w3(sl, KD, 256)
        r_V = res("V")
        first = True
        for blk in range(5):
            for ab in range(2):
                tok0 = blk * 128 + (64 if ab else 0)
                M = 128 if tok0 + 128 <= 128 + NT else 64
                bk, rb = bank()
                for k in range(KD):
                    MM(bk[0:M, 0:256], hT[:, k, tok0:tok0 + M], wv[:, k, :], k == 0, k == KD - 1, [rs, r_hT], [rb])
                Vt = Vb if ab else Va
                src = bk[0:M, 0:256].rearrange("p (j d) -> p j d", d=64)
                wr = [r_V]
                first = False
                COPY("act", Vt[0:M, blk, :, 0:64], src, [rb], wr)
                COPY("dve", Vt[0:M, blk, :, 64:128], src, [rb], [r_V])
        dump("Va1", Va[:, 1, :, :], [r_V], t, l)
        if LV_:
            bk, rb = bank()
            for k in range(KD):
                MM(bk[:, 0:256], hT[:, k, NT:NT + 128], wv[:, k, :], k == 0, k == KD - 1, [rs, r_hT], [rb])
            COPY("dve", ostage[:, 512:768], bk[:, 0:256], [rb], [res("oT")])
            DMA("sp", vp_d[l, :, :], ostage[:, 512:768], semc("ost"), [res("oT")], [res("o_vp")])
        COPY("pool", hhist[l][:, :, :], hT[:, :, NT:NT + 128], [r_hT], [res(f"hhist{l}")])
        r_oT = res("oT")
        rsm = res("small")
        for c in range(ch0, 8):
            for j in range(4):
                bS, rS = bank()
                rhs = qbd[:, c, 2 * j:2 * j + 2, :, :].rearrange("p a b t -> p (a b t)")
                MM(bS[:, 0:256], kd[:, j, 64 * c:64 * c + 128], rhs, True, True, [r_kd, r_qbd], [rS])
                MM(bS[0:64, 256:512], kd[:, j, 128 + 64 * c:192 + 64 * c], rhs, True, True, [r_kd, r_qbd], [rS])
                if tile0:
                    bp = bc = small[:, 1:2]
                elif t == 1 and c == 0:
                    bp = small[:, 1:2]; bc = small[:, 0:1]
                elif t == 1 and c == 1:
                    bp = small[:, 2:3]; bc = small[:, 0:1]
                else:
                    bp = bc = small[:, 0:1]
                et, ret = scrb()
                ACT(et[:, 0:256], bS[:, 0:256], AF.Exp, [rS, rsm], [ret], scale=0.125, bias=bp)
                ACT(et[0:64, 256:512], bS[0:64, 256:512], AF.Exp, [rS, rsm], [ret], scale=0.125, bias=bc[0:64, :])
                if c % 2 == 0:
                    Vp = Va[:, c // 2, j, :]; Vc = Va[0:64, c // 2 + 1, j, :]
                else:
                    Vp = Vb[:, (c - 1) // 2, j, :]; Vc = Vb[0:64, (c + 1) // 2, j, :]
                bO, rO = bank()
                MM(bO[:, 0:256], Vp, et[:, 0:256], True, False, [r_V, ret], [rO])
                MM(bO[:, 0:256], Vc, et[0:64, 256:512], False, True, [r_V, ret], [rO])
                MM(bO[:, 256:512], ones_bf[:, :], et[:, 0:256], True, False, [r_ones, ret], [rO])
                MM(bO[:, 256:512], ones_bf[0:64, :], et[0:64, 256:512], False, True, [r_ones, ret], [rO])
                den, rden = scr()
                TT("dve", den[:, 0:256].rearrange("p (g t) -> p g t", t=64), bO[:, 256:512].rearrange("p (g t) -> p g t", t=64),
                   small[:, 16 + 4 * j:20 + 4 * j].unsqueeze(2).to_broadcast([128, 4, 64]), ALU.add, [rO, rsm], [rden])
                RECIP(den[:, 0:256], den[:, 0:256], [rden], [rden])
                for hf in range(2):
                    p0 = hf * 64
                    o_in = bO[p0:p0 + 64, 0:256].rearrange("p (a g t) -> p a g t", g=2, t=64)[:, :, hf, :]
                    d_in = den[p0:p0 + 64, 0:256].rearrange("p (a g t) -> p a g t", g=2, t=64)[:, :, hf, :]
                    TT("dve", oT[p0:p0 + 64, 2 * j:2 * j + 2, 64 * c:64 * c + 64], o_in, d_in, ALU.mult, [rO, rden], [r_oT])
        dump("o0", oT[:, 0, :], [r_oT], t, l)
        dump("o5", oT[:, 5, :], [r_oT], t, l)
        dump("cn5", cn[:, 5, :], [r_cn], t, l)
        merge_out(l, t, c0, NT)

    def merge_out(l, t, c0, c1):
        N = c1 - c0
        wi = wview(w_in, l)
        r_hT = res("hT"); r_xT = res("xT"); r_cn = res("cn"); r_oT = res("oT")
        r_u = rl("u", CCH)
        r_mix = res("mix")
        wco = wview(w_co, l); wao = wview(w_ao, l)
        firstmix = True
        for m in range(KD):
            slA, rsA = wload([(lambda s: w3(s, KD, 128), wi[:, :, O_GC + m * 128:O_GC + (m + 1) * 128]),
                              (lambda s: w3(s, KD, 128, 2048), wi[:, :, O_GA + m * 128:O_GA + (m + 1) * 128])])
            slB, rsB = wload([(lambda s: w3(s, CCH, 128), wco[:, :, m * 128:(m + 1) * 128]),
                              (lambda s: w3(s, CCH, 128, 1024), wao[:, :, m * 128:(m + 1) * 128])])
            wgc = w3(slA, KD, 128); wga = w3(slA, KD, 128, 2048)
            wc_ = w3(slB, CCH, 128); wa_ = w3(slB, CCH, 128, 1024)
            b1, rb1 = bank(); b2, rb2 = bank(); b3, rb3 = bank(); b4, rb4 = bank()
            for k in range(KD):
                MM(b1[:, :N], wgc[:, k, :], hT[:, k, 128 + c0:128 + c1], k == 0, k == KD - 1, [rsA, r_hT], [rb1])
            for k in range(CCH):
                MM(b2[:, :N], wc_[:, k, :], cn[:, k, c0:c1], k == 0, k == CCH - 1, [rsB, r_cn], [rb2])
            for k in range(KD):
                MM(b3[:, :N], wga[:, k, :], hT[:, k, 128 + c0:128 + c1], k == 0, k == KD - 1, [rsA, r_hT], [rb3])
            for k in range(CCH):
                MM(b4[:, :N], wa_[:, k, :], oT[:, k, c0:c1], k == 0, k == CCH - 1, [rsB, r_oT], [rb4])
            s1, rs1 = scr(); s2, rs2 = scr()
            ACT(s1[:, :N], b1[:, :N], AF.Sigmoid, [rb1], [rs1])
            ACT(s2[:, :N], b3[:, :N], AF.Sigmoid, [rb3], [rs2])
            if m == 0:
                dump("sgc0", s1[:, :N], [rs1], t, l)
                dump("sga0", s2[:, :N], [rs2], t, l)
            TT("dve", s1[:, :N], b2[:, :N], s1[:, :N], ALU.mult, [rb2, rs1], [rs1])
            TT("dve", s2[:, :N], b4[:, :N], s2[:, :N], ALU.mult, [rb4, rs2], [rs2])
            if m == 0:
                dump("m1", s1[:, :N], [rs1], t, l)
                dump("m2", s2[:, :N], [rs2], t, l)
            wr = [r_mix] + r_u if firstmix else [r_mix]
            firstmix = False
            TT("pool", mix[:, m, c0:c1], s1[:, :N], s2[:, :N], ALU.add, [rs1, rs2], wr)
        dump("mix0", mix[:, 0, :], [r_mix], t, l)
        wo = wview(w_o, l)
        for mp in range(KD // 2):
            sl, rs = wload([(lambda s: w3(s, KD, 256), wo[:, :, mp * 256:(mp + 1) * 256])])
            ww = w3(sl, KD, 256)
            for h2 in range(2):
                m = mp * 2 + h2
                bk, rb = bank()
                for k in range(KD):
                    MM(bk[:, :N], ww[:, k, h2 * 128:(h2 + 1) * 128], mix[:, k, c0:c1], k == 0, k == KD - 1, [rs, r_mix], [rb])
                TT("dve", xT[:, m, c0:c1], xT[:, m, c0:c1], bk[:, :N], ALU.add, [r_xT, rb], [r_xT])

    def ffn(l, t, c1=NT):
        dump("x1_0", xT[:, 0, :], [res("xT")], t, l)
        tile0 = (t == 0)
        c0 = 128 * l if tile0 else 0
        N = c1 - c0
        r_hT = res("hT"); r_xT = res("xT")
        rmsnorm(l, V_G2, c0, False, c1)
        wg = wview(w_gu, l)
        r_act = res("act")
        firsta = True
        for f in range(FCH):
            sl, rs = wload([(lambda s: w3(s, KD, 128), wg[:, :, f * 128:(f + 1) * 128]),
                            (lambda s: w3(s, KD, 128, 2048), wg[:, :, DFF + f * 128:DFF + (f + 1) * 128])])
            wgg = w3(sl, KD, 128); wuu = w3(sl, KD, 128, 2048)
            bG, rG = bank(); bU, rU = bank()
            for k in range(KD):
                MM(bG[:, :N], wgg[:, k, :], hT[:, k, 128 + c0:128 + c1], k == 0, k == KD - 1, [rs, r_hT], [rG])
            for k in range(KD):
                MM(bU[:, :N], wuu[:, k, :], hT[:, k, 128 + c0:128 + c1], k == 0, k == KD - 1, [rs, r_hT], [rU])
            sg, rsg = scr()
            ACT(sg[:, :N], bG[:, :N], AF.Silu, [rG], [rsg])
            wr = ([r_act, res("mix"), res("qbd"), res("cn"), res("oT"), res("kd"), res("V")] + rl("u", CCH) + rl("c", CCH)) if firsta else [r_act]
            firsta = False
            TT("dve", actT[:, f, c0:c1], bU[:, :N], sg[:, :N], ALU.mult, [rU, rsg], wr)
        dump("act0", actT[:, 0, :], [r_act], t, l)
        wd = wview(w_dn, l)
        for m in range(KD):
            sl0, rs0 = wload([(lambda s: w3(s, 22, 128), wd[:, 0:22, m * 128:(m + 1) * 128])])
            sl1, rs1 = wload([(lambda s: w3(s, 22, 128), wd[:, 22:44, m * 128:(m + 1) * 128])])
            w0 = w3(sl0, 22, 128); w1 = w3(sl1, 22, 128)
            bk, rb = bank()
            for k in range(FCH):
                ww = w0[:, k, :] if k < 22 else w1[:, k - 22, :]
                MM(bk[:, :N], ww, actT[:, k, c0:c1], k == 0, k == FCH - 1, [rs0 if k < 22 else rs1, r_act], [rb])
            TT("dve", xT[:, m, c0:c1], xT[:, m, c0:c1], bk[:, :N], ALU.add, [r_xT, rb], [r_xT])

    def load_tile(t):
        r_xT = res("xT")
        DMA("sp", cosT[:, :], ropec[:, t * NT:(t + 1) * NT], semc("rope"), (), [res("rope")])
        DMA("sp", sinT[:, :], ropes[:, t * NT:(t + 1) * NT], semc("rope"), (), [res("rope")])
        for blk in range(4):
            xs = xstage[blk % 2]; rxs = res(f"xstage{blk % 2}")
            DMA("sp", xs[:, :], xin[t * NT + blk * 128:t * NT + (blk + 1) * 128, :], semc(f"xst{blk % 2}"), (),
                all_phase() if blk < 2 else [rxs])
            for g in range(4):
                bk, rb = bank()
                for k4 in range(4):
                    k = g * 4 + k4
                    TR(bk[:, k4 * 128:(k4 + 1) * 128], xs[:, k * 128:(k + 1) * 128], ident, [rxs, r_consts], [rb])
                COPY("act" if g % 2 == 0 else "dve", xT[:, g * 4:(g + 1) * 4, blk * 128:(blk + 1) * 128],
                     bk[:, :].rearrange("p (k t) -> p k t", t=128), [rb], [r_xT])

    def store_tile(t):
        r_xT = res("xT")
        for blk in range(4):
            xs = xstage[blk % 2]; rxs = res(f"xstage{blk % 2}")
            for g in range(4):
                bk, rb = bank()
                for k4 in range(4):
                    k = g * 4 + k4
                    TR(bk[:, k4 * 128:(k4 + 1) * 128], xT[:, k, blk * 128:(blk + 1) * 128], ident, [r_xT, r_consts], [rb])
                wr = all_phase() if (blk < 2 and g == 0) else [rxs]
                COPY("act" if g % 2 == 0 else "dve", xs[:, g * 512:(g + 1) * 512], bk[:, :], [rb], wr)
            row0 = (t - 1) * NT + blk * 128
            DMA("sp", y_d[row0:row0 + 128, :], xs[:, :], semc(f"yst{blk % 2}"), [rxs], [res(f"o_y{t}_{blk}")])
            out_res.append(res(f"o_y{t}_{blk}"))


    def load_sample(from_y=False):
        r_xT = res("xT")
        c0s = NTILES * NT
        DMA("sp", cosT[:, 0:NS], ropec[:, c0s:c0s + NS], semc("rope"), (), [res("rope")])
        DMA("sp", sinT[:, 0:NS], ropes[:, c0s:c0s + NS], semc("rope"), (), [res("rope")])
        xs = xstage[0]; rxs = res("xstage0")
        if from_y:
            DMA("sp", xs[0:NS, :], y_d[SEQ_PER:SEQ_PER + NS, :], semc("xst0"), [res("o_ys")], all_phase())
        else:
            DMA("sp", xs[0:NS, :], xin[c0s:c0s + NS, :], semc("xst0"), (), all_phase())
        for g in range(4):
            bk, rb = bank()
            for k4 in range(4):
                k = g * 4 + k4
                TR(bk[:, k4 * NS:(k4 + 1) * NS], xs[0:NS, k * 128:(k + 1) * 128], ident[0:NS, 0:NS], [rxs, r_consts], [rb])
            COPY("act" if g % 2 == 0 else "dve", xT[:, g * 4:(g + 1) * 4, 0:NS],
                 bk[:, 0:4 * NS].rearrange("p (k t) -> p k t", t=NS), [rb], [r_xT])

    def store_sample():
        r_xT = res("xT")
        xs = xstage[0]; rxs = res("xstage0")
        for g in range(4):
            bk, rb = bank()
            for k4 in range(4):
                k = g * 4 + k4
                TR(bk[0:NS, k4 * 128:(k4 + 1) * 128], xT[:, k, 0:NS], ident, [r_xT, r_consts], [rb])
            wr = all_phase() if g == 0 else [rxs]
            COPY("act" if g % 2 == 0 else "dve", xs[0:NS, g * 512:(g + 1) * 512], bk[0:NS, :], [rb], wr)
        DMA("sp", y_d[SEQ_PER:SEQ_PER + NS, :], xs[0:NS, :], semc("yst0"), [rxs], [res("o_ys")])
        if res("o_ys") not in out_res:
            out_res.append(res("o_ys"))

    def mixer_sample(l):
        t = -1
        N = NS
        b_ = l * LV
        wi = wview(w_in, l)
        r_hT = res("hT"); r_xT = res("xT"); r_oT = res("oT"); r_cn = res("cn"); r_kd = res("kd"); r_V = res("V")
        r_u = rl("u", CCH); r_c = rl("c", CCH)
        r_un = res("unew")
        uS = [uT[:, m, 0:BPC * 46].rearrange("p (b t) -> p b t", t=46) for m in range(CCH)]
        kS = kd[:, :, :].rearrange("p a b -> p (a b)")[:, 0:BPC * 4 * 144].rearrange("p (b j t) -> p b j t", b=BPC, j=4)
        qS = qbd[:, 0, :, :, :].rearrange("p a b t -> p (a b t)").rearrange("p (b q g t) -> p b q g t", b=BPC, q=8, g=2)
        SST = int(os.environ.get("SST", "99")) if l >= 1 else 99
        r_ckr = res("ckr"); r_ckd = [res("ckd0"), res("ckd1")]
        if SST <= 1:
            return
        DMA("sp", ostage[0:BPC * 30, :], st_conv[l, :, :, :].rearrange("b t c -> (b t) c"), semc("ost"), (), all_phase() + [r_ckr] + r_ckd)
        npad = 0
        if l >= 1:
            for en_ in ("pe", "act", "dve"):
                for _ in range(npad):
                    S.add(en_, lambda e: e.nop(), (), ())
        while bank_i[0] % 8 != int(os.environ.get("SABANK", "4")):
            bank_i[0] += 1
        for hlf in range(2):
            bk, rb = bank()
            for m4 in range(4):
                m = hlf * 4 + m4
                MM(bk[:, m4 * 120:(m4 + 1) * 120], ostage[0:120, m * 128:(m + 1) * 128], ident[0:120, 0:120], True, True, [r_oT, r_consts], [rb])
            dstA = uT[:, hlf * 4:(hlf + 1) * 4, 0:BPC * 46].rearrange("p m (b t) -> p m b t", t=46)[:, :, :, 0:30]
            srcA = bk[:, 0:480].rearrange("p (m b t) -> p m b t", m=4, t=30)
            COPY("dve", dstA, srcA, [rb], [r_u[hlf * 4 + q_] for q_ in range(4)])
        if l >= 1 and os.environ.get("E2", "0") == "1":
            return
        S.add("sp", lambda e: [e.dma_start(out=convs_d[l, bb, 0:14, :], in_=ostage[bb * 30 + 16:bb * 30 + 30, :]) for bb in range(BPC)],
              [r_oT], [res(f"o_pc{l}")], dma=semc("ost"), ndma=BPC)
        out_res.append(res(f"o_pc{l}"))
        if SST <= 2:
            return
        S.add("sp", lambda e: [e.dma_start(out=ckr[:, bb, :], in_=c_k[l, bb, :, :]) for bb in range(BPC)],
              (), [r_ckr, r_cn], dma=semc("ck"), ndma=BPC)
        S.add("sp", lambda e: [e.dma_start(out=ks_d[l, bb, 0:WIN - DEC_T, :], in_=ckr[DEC_T:WIN, bb, :]) for bb in range(BPC)],
              [r_ckr], [res(f"o_pk{l}")], dma=semc("ck"), ndma=BPC)
        out_res.append(res(f"o_pk{l}"))
        for bb in range(BPC):
            cd_ = ckd[bb % 2]; rcd = r_ckd[bb % 2]
            srck = ckr[:, bb, :].rearrange("p (j d) -> p j d", d=64)
            COPY("dve", cd_[:, :, 0, :], srck, [r_ckr], [rcd])
            COPY("pool", cd_[:, :, 1, :], srck, [r_ckr], [rcd])
            for j in range(4):
                bk, rb = bank()
                MM(bk[:, 0:128], cd_[:, j, :, :].rearrange("p a d -> p (a d)"), ident, True, True, [rcd, r_consts], [rb])
                COPY("act" if j % 2 == 0 else "dve", kS[:, bb, j, 0:128], bk[:, 0:128], [rb], [r_kd])
        if SST <= 3:
            return
        S.add("sp", lambda e: [e.dma_start(out=ostage[:, bb * 256:(bb + 1) * 256], in_=c_v[l, bb, :, :]) for bb in range(BPC)],
              (), [r_oT], dma=semc("ost"), ndma=BPC)
        S.add("sp", lambda e: [e.dma_start(out=vs_d[l, bb, 0:WIN - DEC_T, :], in_=ostage[DEC_T:WIN, bb * 256:(bb + 1) * 256]) for bb in range(BPC)],
              [r_oT], [res(f"o_pv{l}")], dma=semc("ost"), ndma=BPC)
        out_res.append(res(f"o_pv{l}"))
        srcv = ostage[:, :].rearrange("p (b j d) -> p b j d", b=BPC, d=64)
        COPY("act", Va[:, 0:BPC, :, 0:64], srcv, [r_oT], [r_V])
        COPY("dve", Va[:, 0:BPC, :, 64:128], srcv, [r_oT], [r_V])
        if SST <= 4:
            return
        rmsnorm(l, V_G1, 0, False, NS)
        for m in range(CCH):
            sl, rs = wload([(lambda s: w3(s, KD, 128), wi[:, :, O_AL + m * 128:O_AL + (m + 1) * 128]),
                            (lambda s: w3(s, KD, 128, 2048), wi[:, :, O_AG + m * 128:O_AG + (m + 1) * 128])])
            wa = w3(sl, KD, 128); wg = w3(sl, KD, 128, 2048)
            bA, rA = bank(); bG, rG = bank()
            for k in range(KD):
                MM(bA[:, :N], wa[:, k, :], hT[:, k, 128:128 + N], k == 0, k == KD - 1, [rs, r_hT], [rA])
            for k in range(KD):
                MM(bG[:, :N], wg[:, k, :], hT[:, k, 128:128 + N], k == 0, k == KD - 1, [rs, r_hT], [rG])
            sg, rsg = scr()
            ACT(sg[:, :N], bG[:, :N], AF.Sigmoid, [rG], [rsg])
            TT("dve", cT[:, m, 64:128], bA[:, :N], sg[:, :N], ALU.mult, [rA, rsg], [r_un, r_c[m]])
            COPY("pool", uS[m][:, :, 30:46], cT[:, m, 64:128].rearrange("p (b t) -> p b t", t=DEC_T), [r_un, r_c[m]], [r_u[m]])
        for hlf in range(2):
            bk, rb = bank()
            for m4 in range(4):
                m = hlf * 4 + m4
                TR(bk[0:NS, m4 * 128:(m4 + 1) * 128], cT[:, m, 64:128], ident, [r_un, r_c[m], r_consts], [rb])
            COPY("act", ostage[0:NS, hlf * 512:(hlf + 1) * 512], bk[0:NS, :], [rb], [r_oT])
        for bb in range(BPC):
            DMA("sp", convs_d[l, bb, 14:30, :], ostage[bb * DEC_T:(bb + 1) * DEC_T, :], semc("ost"), [r_oT], [res(f"o_cs{l}_{bb}")])
            out_res.append(res(f"o_cs{l}_{bb}"))
        for m in range(CCH):
            wcol = b_ + V_WDW + m * CW
            cdst = cT[:, m, 0:N].rearrange("p (b t) -> p b t", t=DEC_T)
            TS("dve", cdst, uS[m][:, :, 0:DEC_T], vecs[:, wcol:wcol + 1], vecs[:, b_ + V_BDW + m:b_ + V_BDW + m + 1],
               ALU.mult, ALU.add, [r_u[m], r_vecs], [r_c[m]])
            for j in range(1, CW):
                STT("dve", cdst, uS[m][:, :, j:j + DEC_T], vecs[:, wcol + j:wcol + j + 1], cdst,
                    ALU.mult, ALU.add, [r_u[m], r_c[m], r_vecs], [r_c[m]])
        b1, rb1 = bank(); b2, rb2 = bank()
        for m in range(CCH):
            MM(b1[:, :N], ones_f, cT[:, m, 0:N], m == 0, m == CCH - 1, [r_c[m], r_consts], [rb1])
        for m in range(CCH):
            sq, rsq = scr()
            ACT(sq[:, :N], cT[:, m, 0:N], AF.Square, [r_c[m]], [rsq])
            MM(b2[:, :N], ones_f, sq[:, :N], m == 0, m == CCH - 1, [rsq, r_consts], [rb2])
        mean, rmean = scr()
        TS("dve", mean[:, :N], b1[:, :N], 1.0 / CD, None, ALU.mult, None, [rb1], [rmean])
        msq, rmsq = scr()
        TT("dve", msq[:, :N], mean[:, :N], mean[:, :N], ALU.mult, [rmean], [rmsq])
        var, rvar = scr()
        STT("dve", var[:, :N], b2[:, :N], 1.0 / CD, msq[:, :N], ALU.mult, ALU.subtract, [rb2, rmsq], [rvar])
        ACT(var[:, :N], var[:, :N], AF.Sqrt, [rvar, res("small")], [rvar], bias=small[:, 8:9])
        RECIP(var[:, :N], var[:, :N], [rvar], [rvar])
        STT("dve", msq[:, :N], mean[:, :N], -1.0, var[:, :N], ALU.mult, ALU.mult, [rmean, rvar], [rmsq])
        held = {id(mean), id(msq), id(var)}
        free_sf = [i for i in range(NSF) if id(sf[i]) not in held]
        for m in range(CCH):
            fi = free_sf[m % len(free_sf)]
            tt_, rtt = sf[fi], res(f"sf{fi}")
            TT("dve", tt_[:, :N], cT[:, m, 0:N], var[:, :N], ALU.mult, [r_c[m], rvar], [rtt])
            TT("dve", tt_[:, :N], tt_[:, :N], msq[:, :N], ALU.add, [rtt, rmsq], [rtt])
            ACT(cn[:, m, 0:N], tt_[:, :N], AF.Silu, [rtt, r_vecs], [r_cn] + ([r_ckr] + r_ckd if m == 0 else []),
                scale=vecs[:, b_ + V_LNG + m:b_ + V_LNG + m + 1], bias=vecs[:, b_ + V_LNB + m:b_ + V_LNB + m + 1])
        r_qbd = res("qbd")
        S.add("pool", lambda e: e.memset(qbd[:, 0, :, :, :], 0.0), (), [r_qbd, r_un] + r_c)
        for qp in range(4):
            sl, rs = wload([(lambda s: w3(s, KD, 256), wi[:, :, O_Q + qp * 256:O_Q + (qp + 1) * 256])])
            wq = w3(sl, KD, 256)
            for h2 in range(2):
                qc = qp * 2 + h2
                bk, rb = bank()
                for k in range(KD):
                    MM(bk[:, :N], wq[:, k, h2 * 128:(h2 + 1) * 128], hT[:, k, 128:128 + N], k == 0, k == KD - 1, [rs, r_hT], [rb])

                def dst(t1, t2, rt1, rt2, qc=qc):
                    TT("pool", qS[0:64, :, qc, 0, :], t1[0:64, :N].rearrange("p (b t) -> p b t", t=DEC_T),
                       t2[0:64, :N].rearrange("p (b t) -> p b t", t=DEC_T), ALU.add, [rt1, rt2], [r_qbd])
                    TT("pool", qS[64:128, :, qc, 1, :], t1[64:128, :N].rearrange("p (b t) -> p b t", t=DEC_T),
                       t2[64:128, :N].rearrange("p (b t) -> p b t", t=DEC_T), ALU.add, [rt1, rt2], [r_qbd])
                qk_chain(l, bk, rb, 0, V_GQ, False, dst, False, None, NS)
        wk = wi[:, :, O_K:O_K + 256]
        for jp in range(2):
            parts = []
            for jj in range(2):
                j = jp * 2 + jj
                for dup in range(2):
                    parts.append((lambda s, o=(jj * 2 + dup) * 64: w3(s, KD, 256)[:, :, o:o + 64], wk[:, :, j * 64:(j + 1) * 64]))
            sl, rs = wload(parts)
            wkk = w3(sl, KD, 256)
            for jj in range(2):
                j = jp * 2 + jj
                bk, rb = bank()
                for k in range(KD):
                    MM(bk[:, :N], wkk[:, k, jj * 128:(jj + 1) * 128], hT[:, k, 128:128 + N], k == 0, k == KD - 1, [rs, r_hT], [rb])

                def dstk(t1, t2, rt1, rt2, j=j):
                    TT("pool", kS[:, :, j, 128:144], t1[:, :N].rearrange("p (b t) -> p b t", t=DEC_T),
                       t2[:, :N].rearrange("p (b t) -> p b t", t=DEC_T), ALU.add, [rt1, rt2], [r_kd])
                    TT("pool", kf[:, j, 0:N], t1[:, :N], t2[:, :N], ALU.add, [rt1, rt2], [res("kf")])
                qk_chain(l, bk, rb, 0, V_GK, True, dstk, False, j, NS)
        for j in range(4):
            bk, rb = bank()
            TR(bk[0:NS, 0:128], kf[:, j, 0:NS], ident, [res("kf"), r_consts], [rb])
            COPY("act", ostage[0:NS, j * 64:(j + 1) * 64], bk[0:NS, 0:64], [rb], [r_oT])
        for bb in range(BPC):
            DMA("sp", ks_d[l, bb, WIN - DEC_T:WIN, :], ostage[bb * DEC_T:(bb + 1) * DEC_T, 0:256], semc("ost"), [r_oT], [res(f"o_ksn{l}_{bb}")])
            out_res.append(res(f"o_ksn{l}_{bb}"))
        sl, rs = wload([(lambda s: w3(s, KD, 256), wi[:, :, O_V:O_V + 256])])
        wv = w3(sl, KD, 256)
        for bp in range(2):
            bk, rb = bank()
            for b2_ in range(2):
                bb = bp * 2 + b2_
                for k in range(KD):
                    MM(bk[0:DEC_T, b2_ * 256:(b2_ + 1) * 256], hT[:, k, 128 + bb * DEC_T:128 + (bb + 1) * DEC_T], wv[:, k, :],
                       k == 0, k == KD - 1, [rs, r_hT], [rb])
            src = bk[0:DEC_T, :].rearrange("p (b j d) -> p b j d", b=2, d=64)
            COPY("dve", ostage[0:DEC_T, bp * 512:(bp + 1) * 512], bk[0:DEC_T, :], [rb], [r_oT])
            srcs = ostage[0:DEC_T, bp * 512:(bp + 1) * 512].rearrange("p (b j d) -> p b j d", b=2, d=64)
            COPY("pool", Vb[0:DEC_T, bp * 2:bp * 2 + 2, :, 0:64], srcs, [r_oT], [r_V])
            COPY("pool", Vb[0:DEC_T, bp * 2:bp * 2 + 2, :, 64:128], srcs, [r_oT], [r_V])
        DMA("sp", vs_d[l, :, WIN - DEC_T:WIN, :].rearrange("b t c -> t b c"),
            ostage[0:DEC_T, :].rearrange("p (b c) -> p b c", c=256), semc("ost"), [r_oT], [res(f"o_vsn{l}")])
        out_res.append(res(f"o_vsn{l}"))
        rsm = res("small")
        for j in range(4):
            bS, rS = bank()
            for bb in range(BPC):
                rhs = qS[:, bb, 2 * j:2 * j + 2, :, :].rearrange("p a g t -> p (a g t)")
                MM(bS[:, bb * 128:bb * 128 + 64], kS[:, bb, j, 0:128], rhs, True, True, [r_kd, r_qbd], [rS])
                MM(bS[0:DEC_T, bb * 128 + 64:bb * 128 + 128], kS[:, bb, j, 128:144], rhs, True, True, [r_kd, r_qbd], [rS])
            et, ret = scrb()
            bS3 = bS[:, :].rearrange("p (b c) -> p b c", c=128)
            et3 = et[:, :].rearrange("p (b c) -> p b c", c=128)
            ACT(et3[:, :, 0:64], bS3[:, :, 0:64], AF.Exp, [rS, rsm], [ret], scale=0.125, bias=small[:, 0:1])
            ACT(et3[0:DEC_T, :, 64:128], bS3[0:DEC_T, :, 64:128], AF.Exp, [rS, rsm], [ret], scale=0.125, bias=small[0:DEC_T, 0:1])
            bO, rO = bank()
            for bb in range(BPC):
                c_ = bb * 128
                MM(bO[:, c_:c_ + 64], Va[:, bb, j, :], et[:, c_:c_ + 64], True, False, [r_V, ret], [rO])
                MM(bO[:, c_:c_ + 64], Vb[0:DEC_T, bb, j, :], et[0:DEC_T, c_ + 64:c_ + 128], False, True, [r_V, ret], [rO])
                MM(bO[:, c_ + 64:c_ + 128], ones_bf[:, :], et[:, c_:c_ + 64], True, False, [r_ones, ret], [rO])
                MM(bO[:, c_ + 64:c_ + 128], ones_bf[0:DEC_T, :], et[0:DEC_T, c_ + 64:c_ + 128], False, True, [r_ones, ret], [rO])
            den, rden = scr()
            bO5 = bO[:, :].rearrange("p (b h c t) -> p b h c t", b=BPC, h=2, c=4)
            den4 = den[:, 0:256].rearrange("p (b c t) -> p b c t", b=BPC, c=4)
            TT("dve", den4, bO5[:, :, 1, :, :],
               small[:, 16 + 4 * j:20 + 4 * j].unsqueeze(1).unsqueeze(3).to_broadcast([128, BPC, 4, DEC_T]), ALU.add, [rO, rsm], [rden])
            RECIP(den[:, 0:256], den[:, 0:256], [rden], [rden])
            for hf in range(2):
                p0 = hf * 64
                o_in = bO[p0:p0 + 64, :].rearrange("p (b h a g t) -> p b h a g t", b=BPC, h=2, a=2, g=2)[:, :, 0, :, hf, :]
                d_in = den[p0:p0 + 64, 0:256].rearrange("p (b a g t) -> p b a g t", b=BPC, a=2, g=2)[:, :, :, hf, :]
                o_out = oT[p0:p0 + 64, 2 * j:2 * j + 2, 0:N].rearrange("p a (b t) -> p b a t", t=DEC_T)
                TT("dve", o_out, o_in, d_in, ALU.mult, [rO, rden], [r_oT])
        merge_out(l, t, 0, NS)

    for t in range(ntiles):
        load_tile(t)
        for l in range(depth):
            layer_setup(l)
            if do_mixer:
                mixer(l, t)
            if do_ffn:
                ffn(l, t)
        if t >= 1:
            store_tile(t)
    if do_sample:
        load_sample()
        for l in range(depth):
            if l >= 1:
                store_sample()
                load_sample(from_y=True)
            layer_setup(l)
            mixer_sample(l)
            ffn(l, -1, NS)
        store_sample()
    if do_mixer and ntiles == NTILES and do_last:
        for n, f in (("o_convp", 'LU'), ("o_kp", 'LK'), ("o_vp", 'LVV')):
            if os.environ.get(f, '1') == '1':
                out_res.append(res(n))
    S.add("sp", lambda e: e.nop() if False else None, out_res, ())
    fin = S.q["sp"].pop()
    final_deps = fin.deps

    S.assign()

    def emit(en):
        def f(e):
            S.emit_one(en, e)
            if en == "sp":
                done = {}
                for d in final_deps:
                    k = id(d.sem)
                    if d.val > done.get(k, (None, 0))[1]:
                        done[k] = (d.sem, d.val)
                for s_, v in done.values():
                    e.wait_ge(s_, v)
        return f
    with nc.Block() as block:
        block.tensor(emit("pe")); block.scalar(emit("act")); block.vector(emit("dve"))
        block.gpsimd(emit("pool")); block.sync(emit("sp"))
    print("ops:", {e: len(S.q[e]) for e in S.ENGS}, "sbuf hi", hi[0])
    nc._dbg_names = dbg_names
    return nc


def _host_tables(core):
    half = 8
    inv_freq = (500000.0 ** (-np.arange(half, dtype=np.float32) * 2.0 / 16)).astype(np.float32)
    pos_p = (np.arange(NTILES * NT, dtype=np.float32) + np.float32(core * SEQ_PER - HALO))
    pos_s = np.tile(np.arange(DEC_T, dtype=np.float32) + np.float32(PAST), BPC)
    pos = np.concatenate([pos_p, pos_s]).astype(np.float32)
    ang = (pos[None, :] * inv_freq[:, None]).astype(np.float32)
    c8 = np.cos(ang).astype(np.float32); s8 = np.sin(ang).astype(np.float32)
    T = pos.shape[0]
    cos64 = np.ones((64, T), np.float32); sin64 = np.zeros((64, T), np.float32)
    cos64[0:8] = c8; cos64[8:16] = c8
    sin64[0:8] = -s8; sin64[8:16] = s8
    return np.concatenate([cos64, cos64], 0), np.concatenate([sin64, sin64], 0)


def _consts():
    c = np.zeros((128, 512), np.float32)
    c[:, 0:128] = np.eye(128, dtype=np.float32)
    Rm = np.zeros((128, 128), np.float32)
    for m in range(128):
        d = m % 64
        if d < 8:
            Rm[m + 8, m] = 1.0
        elif d < 16:
            Rm[m - 8, m] = 1.0
    c[:, 128:256] = Rm
    c[:, 256:384] = 1.0
    c[0:64, 384:448] = 1.0
    c[64:128, 448:512] = 1.0
    return c


def _vecs(inp, core):
    v = np.zeros((128, NV), np.float32)

    def fm(a, n):
        return np.ascontiguousarray(a.reshape(n, 128).T)
    for l in range(DEPTH):
        b = l * LV
        v[:, b + V_G1:b + V_G1 + 16] = fm(inp["norm_mix_g"][l], 16)
        v[:, b + V_G2:b + V_G2 + 16] = fm(inp["norm_ffn_g"][l], 16)
        v[:, b + V_BDW:b + V_BDW + 8] = fm(inp["b_dw"][l], 8)
        v[:, b + V_LNG:b + V_LNG + 8] = fm(inp["conv_ln_g"][l], 8)
        v[:, b + V_LNB:b + V_LNB + 8] = fm(inp["conv_ln_b"][l], 8)
        wd = inp["w_dw"][l]
        for m in range(8):
            v[:, b + V_WDW + m * CW:b + V_WDW + (m + 1) * CW] = wd[:, m * 128:(m + 1) * 128].T
        v[:, b + V_GQ] = np.tile(inp["q_norm_g"][l], 2)
        v[:, b + V_GK] = np.tile(inp["k_norm_g"][l], 2)
        v[:, b + V_GQR:b + V_GQR + 64] = inp["q_norm_g"][l][None, :]
        v[:, b + V_GKR:b + V_GKR + 64] = inp["k_norm_g"][l][None, :]
        v[:, b + V_SNK:b + V_SNK + 16] = inp["sinks"][l][None, :]
    valid = 0.0 if core == 0 else 1.0
    mb = 0.0 if core != 0 else -30000.0
    v[:, V_CM] = valid
    v[:, V_CM + 2] = mb
    v[0:64, V_CM + 3] = mb
    return v


_NC_CACHE = {}


def kernel(**inputs):
    inp = {k: np.asarray(v) for k, v in inputs.items()}
    xp = inp["x_prompt"][0]
    xs = inp["x_sample"]
    if "nc" not in _NC_CACHE:
        _NC_CACHE["nc"] = build_program()
    nc = _NC_CACHE["nc"]
    consts = _consts()
    in_maps = []
    for c in range(NCORES):
        xin = np.zeros((NTILES * NT + NS, D), np.float32)
        lo = c * SEQ_PER - HALO
        if lo < 0:
            xin[HALO:HALO + SEQ_PER] = xp[0:SEQ_PER]
        else:
            xin[0:NTILES * NT] = xp[lo:lo + NTILES * NT]
        xin[NTILES * NT:] = xs[c * BPC:(c + 1) * BPC].reshape(NS, D)
        rc, rs = _host_tables(c)
        in_maps.append({
            "xin": xin, "ropec": rc, "ropes": rs, "vecs": _vecs(inp, c), "consts": consts,
            "w_in": inp["w_in"], "w_conv_out": inp["w_conv_out"], "w_attn_out": inp["w_attn_out"],
            "w_out": inp["w_out"], "w_gate_up": inp["w_gate_up"], "w_down": inp["w_down"],
            "state_conv": np.ascontiguousarray(inp["state_conv"][:, c * BPC:(c + 1) * BPC]),
            "cache_k": np.ascontiguousarray(inp["cache_k"][:, c * BPC:(c + 1) * BPC].reshape(DEPTH, BPC, WIN, 256)),
            "cache_v": np.ascontiguousarray(inp["cache_v"][:, c * BPC:(c + 1) * BPC].reshape(DEPTH, BPC, WIN, 256)),
        })
    res = run_bass_kernel_spmd(nc, in_maps, core_ids=list(range(NCORES)))
    R = res.results
    y_prompt = np.concatenate([R[c]["y"][0:SEQ_PER] for c in range(NCORES)], 0)[None]
    y_sample = np.concatenate([R[c]["y"][SEQ_PER:].reshape(BPC, DEC_T, D) for c in range(NCORES)], 0)
    conv_p = R[NCORES - 1]["convp"][:, None]
    k_p = R[NCORES - 1]["kp"].reshape(DEPTH, 1, WIN, 4, 64)
    v_p = R[NCORES - 1]["vp"].reshape(DEPTH, 1, WIN, 4, 64)
    conv_s = np.concatenate([R[c]["convs"] for c in range(NCORES)], 1)
    k_s = np.concatenate([R[c]["ks"].reshape(DEPTH, BPC, WIN, 4, 64) for c in range(NCORES)], 1)
    v_s = np.concatenate([R[c]["vs"].reshape(DEPTH, BPC, WIN, 4, 64) for c in range(NCORES)], 1)
    return (y_prompt.astype(np.float32), y_sample.astype(np.float32), conv_p.astype(np.float32),
            k_p.astype(np.float32), v_p.astype(np.float32), conv_s.astype(np.float32),
            k_s.astype(np.float32), v_s.astype(np.float32))
```

```python
import os
import numpy as np
import concourse.bass as bass
import concourse.mybir as mybir
from concourse.bass_utils import run_bass_kernel_spmd

F32, BF16 = mybir.dt.float32, mybir.dt.bfloat16
AF = mybir.ActivationFunctionType
ALU = mybir.AluOpType
AX = mybir.AxisListType

NCORES = 8
D = 2048; KD = 16; DEPTH = 4; CD = 1024; CCH = 8; DFF = 5632; FCH = 44; IN_DIM = 7680
NT = 512; NTILES = 5; HALO = 512; SEQ = 16384; SEQ_PER = 2048
NS = 64; DEC_B = 32; DEC_T = 16; BPC = 4; PAST = 4096; WIN = 128; CW = 31
O_AL, O_AG, O_Q, O_K, O_V, O_GC, O_GA = 0, 1024, 2048, 3072, 3328, 3584, 5632
EPS = 1e-6
V_G1, V_G2, V_BDW, V_LNG, V_LNB, V_WDW, V_GQ, V_GK, V_GQR, V_GKR, V_SNK = 0, 16, 32, 40, 48, 56, 304, 305, 306, 370, 434
LV = 450
V_CM = DEPTH * LV
NV = V_CM + 4
SAMPLE = True
SB_BASE = 16512
SB_LIMIT = 229376


class Res:
    __slots__ = ("name", "last_w", "readers")

    def __init__(self, name):
        self.name = name; self.last_w = None; self.readers = {}


class SemC:
    __slots__ = ("sem", "count")

    def __init__(self, sem):
        self.sem = sem; self.count = 0


class Op:
    __slots__ = ("eng", "fn", "deps", "signal", "sem", "val", "dma", "ndma", "seq")


class Sched:
    ENGS = ("pe", "act", "dve", "pool", "sp")

    def __init__(self):
        self.q = {e: [] for e in self.ENGS}
        self.esem = {}
        self.seq = 0

    def add(self, eng, fn, reads=(), writes=(), dma=None, ndma=1):
        op = Op(); op.eng = eng; op.fn = fn; op.signal = False; op.dma = dma; op.ndma = ndma
        op.sem = None; op.val = None; self.seq += 1; op.seq = self.seq
        deps = {}

        def add_dep(d):
            if d is None:
                return
            if d.dma is None and dma is None and d.eng == eng and eng == "pe":
                return
            deps[id(d)] = d
        for r in reads:
            add_dep(r.last_w)
        for w in writes:
            add_dep(w.last_w)
            for d in w.readers.values():
                add_dep(d)
        op.deps = list(deps.values())
        for d in op.deps:
            d.signal = True
        key = eng if dma is None else ("dma", id(dma))
        for r in reads:
            r.readers[key] = op
        for w in writes:
            w.last_w = op; w.readers = {}
        if dma is not None:
            dma.count += 16 * ndma
            op.sem = dma.sem; op.val = dma.count
        self.q[eng].append(op)
        return op

    def assign(self):
        for e in self.ENGS:
            c = 0
            for op in self.q[e]:
                if op.dma is None and op.signal:
                    c += 1; op.sem = self.esem[e]; op.val = c

    def emit_one(self, e, eng):
        waited = {}
        for op in self.q[e]:
            need = {}
            for d in op.deps:
                k = id(d.sem)
                if d.val > waited.get(k, 0) and d.val > need.get(k, (None, 0))[1]:
                    need[k] = (d.sem, d.val)
            for k, (s, v) in need.items():
                eng.wait_ge(s, v); waited[k] = v
            ins = op.fn(eng)
            if op.dma is not None:
                if not isinstance(ins, (list, tuple)):
                    ins = [ins]
                assert len(ins) == op.ndma
                for i in ins:
                    i.then_inc(op.sem, 16)
            elif op.signal:
                ins.then_inc(op.sem, 1)


def build_program(ntiles=NTILES, depth=DEPTH, do_mixer=True, do_ffn=True, mix_stop=99, dbg=0, do_last=True, do_sample=True):
    nc = bass.Bass("TRN2", target_bir_lowering=False)

    def din(name, shape):
        return nc.dram_tensor(name, list(shape), F32, kind="ExternalInput").ap()

    def dout(name, shape):
        return nc.dram_tensor(name, list(shape), F32, kind="ExternalOutput").ap()
    xin = din("xin", [NTILES * NT + NS, D])
    ropec = din("ropec", [128, NTILES * NT + NS])
    ropes = din("ropes", [128, NTILES * NT + NS])
    vecs_d = din("vecs", [128, NV])
    consts_d = din("consts", [128, 512])
    w_in = din("w_in", [depth, D, IN_DIM])
    w_co = din("w_conv_out", [depth, CD, D])
    w_ao = din("w_attn_out", [depth, CD, D])
    w_o = din("w_out", [depth, D, D])
    w_gu = din("w_gate_up", [depth, D, 2 * DFF])
    w_dn = din("w_down", [depth, DFF, D])
    st_conv = din("state_conv", [DEPTH, BPC, CW - 1, CD])
    c_k = din("cache_k", [DEPTH, BPC, WIN, 256])
    c_v = din("cache_v", [DEPTH, BPC, WIN, 256])
    y_d = dout("y", [SEQ_PER + NS, D])
    convp_d = dout("convp", [DEPTH, CW - 1, CD])
    kp_d = dout("kp", [DEPTH, WIN, 256])
    vp_d = dout("vp", [DEPTH, WIN, 256])
    convs_d = dout("convs", [DEPTH, BPC, CW - 1, CD])
    ks_d = dout("ks", [DEPTH, BPC, WIN, 256])
    vs_d = dout("vs", [DEPTH, BPC, WIN, 256])
    dbg_d = dout("dbg", [24, 128, 512]) if dbg else None
    dbg_n = [0]
    dbg_names = []

    def wview(w, l):
        return w[l].rearrange("(k p) n -> p k n", p=128)

    cur = [SB_BASE]
    hi = [SB_BASE]
    offs = {}

    def alloc(name, shape, dt, at=None):
        nb = int(np.prod(shape[1:])) * (4 if dt == F32 else 2)
        nb = (nb + 63) // 64 * 64
        if at is None:
            off = cur[0]; cur[0] += nb
        else:
            off = at
        hi[0] = max(hi[0], off + nb)
        offs[name] = off
        assert off + nb <= SB_LIMIT, (name, off, nb)
        return nc.alloc_sbuf_tensor_at(name, list(shape), dt, offset=off)

    xT = alloc("xT", [128, KD, NT], F32)
    hT = alloc("hT", [128, KD, 128 + NT], BF16)
    hhist = [alloc(f"hhist{l}", [128, KD, 128], BF16) for l in range(DEPTH)]
    uhist = [alloc(f"uhist{l}", [128, CCH, 30], F32) for l in range(DEPTH)]
    khist = [alloc(f"khist{l}", [128, 4, 128], BF16) for l in range(DEPTH)]
    NSLOT = 4
    wslot = [alloc(f"wslot{i}", [128, 4096], BF16) for i in range(NSLOT)]
    vecs = alloc("vecs", [128, NV], F32)
    consts = alloc("consts", [128, 512], F32)
    ones_bf = alloc("ones_bf", [128, 128], BF16)
    bones_bf = alloc("bones_bf", [128, 128], BF16)
    cosT = alloc("cosT", [128, NT], F32)
    sinT = alloc("sinT", [128, NT], F32)
    small = alloc("small", [128, 64], F32)
    NSF = 6
    sf = [alloc(f"sf{i}", [128, NT], F32) for i in range(NSF)]
    NSB = 3
    sbf = [alloc(f"sbf{i}", [128, NT], BF16) for i in range(NSB)]
    kf = alloc("kf", [128, 4, 128], F32)
    PH = cur[0]
    uT = alloc("uT", [128, CCH, 30 + NT], F32)
    cT = alloc("cT", [128, CCH, NT], F32)
    cn = alloc("cn", [128, CCH, NT], BF16)
    oT = alloc("oT", [128, CCH, NT], BF16)
    kd = alloc("kd", [128, 4, 128 + NT], BF16)
    Va = alloc("Va", [128, 5, 4, 128], BF16)
    Vb = alloc("Vb", [128, 5, 4, 128], BF16)
    PH_END = cur[0]
    mix = alloc("mix", [128, KD, NT], BF16, at=PH)
    qbd = alloc("qbd", [128, 8, 8, 2, 64], BF16, at=PH + (CCH * (30 + NT) * 4 + 63) // 64 * 64)
    actT = alloc("actT", [128, FCH, NT], BF16, at=PH)
    xstage = [alloc(f"xstage{i}", [128, D], F32, at=PH + i * D * 4) for i in range(2)]
    cur[0] = max(cur[0], PH + FCH * NT * 2, PH + 2 * D * 4)
    ckr = alloc("ckr", [128, 4, 256], F32, at=offs["cn"])
    ckd = [alloc(f"ckd{i}", [128, 4, 2, 64], F32, at=offs["cn"] + 4096 + i * 2048) for i in range(2)]
    ostage = alloc("ostage", [128, 1024], F32, at=offs["oT"])
    ident = consts[:, 0:128]
    Rm = consts[:, 128:256]
    ones_f = consts[:, 256:384]

    banks = [nc.alloc_psum_tensor(f"bank{i}", [128, 512], F32) for i in range(8)]

    S = Sched()
    for e_ in S.ENGS:
        S.esem[e_] = nc.alloc_semaphore(f"es_{e_}")
    R = {}

    def res(n):
        r = R.get(n)
        if r is None:
            r = R[n] = Res(n)
        return r

    def rl(prefix, n):
        return [res(f"{prefix}{i}") for i in range(n)]
    bank_i = [0]

    def bank():
        i = bank_i[0] % 8; bank_i[0] += 1
        return banks[i], res(f"bank{i}")
    sf_i = [0]; sb_i = [0]

    def scr():
        i = sf_i[0] % NSF; sf_i[0] += 1
        return sf[i], res(f"sf{i}")

    def scrb():
        i = sb_i[0] % NSB; sb_i[0] += 1
        return sbf[i], res(f"sbf{i}")

    sems = {}

    def semc(name):
        if name not in sems:
            sems[name] = SemC(nc.alloc_semaphore(name))
        return sems[name]

    def ACT(out, in_, func, reads, writes, scale=1.0, bias=0.0):
        S.add("act", lambda e: e.activation(out=out, in_=in_, func=func, bias=bias, scale=scale), reads, writes)

    def TT(eng, out, in0, in1, op, reads, writes):
        S.add(eng, lambda e: e.tensor_tensor(out=out, in0=in0, in1=in1, op=op), reads, writes)

    def STT(eng, out, in0, scalar, in1, op0, op1, reads, writes):
        S.add(eng, lambda e: e.scalar_tensor_tensor(out=out, in0=in0, scalar=scalar, in1=in1, op0=op0, op1=op1), reads, writes)

    def TS(eng, out, in0, s1, s2, op0, op1, reads, writes):
        if s2 is None:
            S.add(eng, lambda e: e.tensor_scalar(out=out, in0=in0, scalar1=s1, scalar2=None, op0=op0), reads, writes)
        else:
            S.add(eng, lambda e: e.tensor_scalar(out=out, in0=in0, scalar1=s1, scalar2=s2, op0=op0, op1=op1), reads, writes)

    def RECIP(out, in_, reads, writes):
        S.add("dve", lambda e: e.reciprocal(out=out, in_=in_), reads, writes)

    def COPY(eng, out, in_, reads, writes):
        if eng == "act":
            ACT(out, in_, AF.Copy, reads, writes)
        else:
            S.add(eng, lambda e: e.tensor_copy(out=out, in_=in_), reads, writes)

    def MM(out, lhsT, rhs, start, stop, reads, writes):
        S.add("pe", lambda e: e.matmul(out, lhsT, rhs, start=start, stop=stop), reads, writes)

    def TR(out, in_, idn, reads, writes):
        S.add("pe", lambda e: e.transpose(out, in_, idn), reads, writes)

    def DMA(eng, out, in_, sem, reads, writes):
        S.add(eng, lambda e: e.dma_start(out=out, in_=in_), reads, writes, dma=sem)

    def dump(name, ap, reads, t, l):
        if not dbg or (t, l) != (1, 0):
            return
        i = dbg_n[0]; dbg_n[0] += 1; dbg_names.append(name)
        sc, rsc = scr()
        shp = list(ap.shape)
        n = int(np.prod(shp[1:]))
        dst = sc[0:shp[0], 0:n]
        if len(shp) == 3:
            dst = dst.rearrange("p (a b) -> p a b", b=shp[2])
        COPY("act", dst, ap, reads, [rsc])
        DMA("sp", dbg_d[i, 0:shp[0], 0:n], sc[0:shp[0], 0:n], semc("dbg"), [rsc], [res(f"o_dbg{i}")])
        out_res.append(res(f"o_dbg{i}"))

    slot_i = [0]

    def wload(parts):
        i = slot_i[0] % NSLOT; slot_i[0] += 1
        sl = wslot[i]; r = res(f"wslot{i}")
        pp = [(d(sl), s) for d, s in parts]
        S.add("pool", lambda e: [e.dma_start(out=d, in_=s) for d, s in pp], (), [r], dma=semc(f"wslot{i}"), ndma=len(pp))
        return sl, r

    def w3(sl, k, n, off=0):
        return sl[:, off:off + k * n].rearrange("p (k n) -> p k n", n=n)

    out_res = []
    PHR = res("PH")

    def all_phase():
        return ([PHR] + [res(n) for n in ("act", "mix", "qbd", "cn", "oT", "kd", "V", "xstage0", "xstage1")]
                + rl("u", CCH) + rl("c", CCH))
    r_vecs = res("vecs"); r_consts = res("consts")
    DMA("sp", vecs[:, :], vecs_d[:, :], semc("setup"), (), [r_vecs])
    DMA("sp", consts[:, :], consts_d[:, :], semc("setup"), (), [r_consts])
    COPY("act", ones_bf[:, :], consts[:, 256:384], [r_consts], [res("ones_bf")])
    COPY("act", bones_bf[:, :], consts[:, 384:512], [r_consts], [res("bones_bf")])
    r_ones = res("ones_bf"); r_bones = res("bones_bf")
    for l in range(DEPTH):
        S.add("dve", lambda e, l=l: e.memset(hhist[l][:, :, :], 0.0), (), [res(f"hhist{l}")])
        S.add("dve", lambda e, l=l: e.memset(uhist[l][:, :, :], 0.0), (), [res(f"uhist{l}")])
        S.add("dve", lambda e, l=l: e.memset(khist[l][:, :, :], 0.0), (), [res(f"khist{l}")])
    S.add("dve", lambda e: e.memset(actT[:, :, :], 0.0), (), all_phase())
    S.add("dve", lambda e: e.memset(Va[:, :, :, :], 0.0), (), [res("V")])
    S.add("dve", lambda e: e.memset(Vb[:, :, :, :], 0.0), (), [res("V")])
    S.add("dve", lambda e: e.memset(kd[:, :, :], 0.0), (), [res("kd")])
    S.add("dve", lambda e: e.memset(oT[:, :, :], 0.0), (), [res("oT")])
    S.add("dve", lambda e: e.memset(hT[:, :, :], 0.0), (), [res("hT")])
    S.add("dve", lambda e: e.memset(xT[:, :, :], 0.0), (), [res("xT")])
    S.add("dve", lambda e: e.memset(kf[:, :, :], 0.0), (), [res("kf")])
    for i in range(NSF):
        S.add("dve", lambda e, i=i: e.memset(sf[i][:, :], 0.0), (), [res(f"sf{i}")])
    for i in range(NSB):
        S.add("dve", lambda e, i=i: e.memset(sbf[i][:, :], 0.0), (), [res(f"sbf{i}")])


    def rmsnorm(l, gcol, c0, tile0, c1=NT):
        N = c1 - c0
        bk, rb = bank()
        for k in range(KD):
            sq, rs = scrb()
            ACT(sq[:, :N], xT[:, k, c0:c1], AF.Square, [res("xT")], [rs])
            MM(bk[:, :N], ones_bf[:, :], sq[:, :N], k == 0, k == KD - 1, [rs, r_ones], [rb])
        sd, rsd = scr()
        ACT(sd[:, :N], bk[:, :N], AF.Sqrt, [rb, res("small")], [rsd], scale=1.0 / D, bias=small[:, 8:9])
        RECIP(sd[:, :N], sd[:, :N], [rsd], [rsd])
        if tile0:
            TS("dve", sd[:, :N], sd[:, :N], vecs[:, V_CM:V_CM + 1], None, ALU.mult, None, [rsd, r_vecs], [rsd])
        for k in range(KD):
            STT("dve", hT[:, k, 128 + c0:128 + c1], xT[:, k, c0:c1], vecs[:, l * LV + gcol + k: l * LV + gcol + k + 1], sd[:, :N],
                ALU.mult, ALU.mult, [res("xT"), rsd, r_vecs], [res("hT")])

    def layer_setup(l):
        b = l * LV
        r = res("small")
        S.add("dve", lambda e: e.memset(small[:, 8:9], EPS), (), [r])
        S.add("dve", lambda e: e.tensor_reduce(out=small[:, 3:4], in_=vecs[:, b + V_GQR:b + V_GQR + 64], axis=AX.X, op=ALU.max, apply_absolute_value=True), [r_vecs], [r])
        S.add("dve", lambda e: e.tensor_reduce(out=small[:, 4:5], in_=vecs[:, b + V_GKR:b + V_GKR + 64], axis=AX.X, op=ALU.max, apply_absolute_value=True), [r_vecs], [r])
        STT("dve", small[:, 0:1], small[:, 3:4], -8.0, small[:, 4:5], ALU.mult, ALU.mult, [r], [r])
        TT("dve", small[:, 1:2], small[:, 0:1], vecs[:, V_CM + 2:V_CM + 3], ALU.add, [r, r_vecs], [r])
        TT("dve", small[:, 2:3], small[:, 0:1], vecs[:, V_CM + 3:V_CM + 4], ALU.add, [r, r_vecs], [r])
        ACT(small[:, 16:32], vecs[:, b + V_SNK:b + V_SNK + 16], AF.Exp, [r, r_vecs], [r], bias=small[:, 0:1])

    def qk_chain(l, bk, rb, c0, gcolidx, is_k, dst_fn, last_tile, j=None, c1=NT):
        N = c1 - c0
        z, rz = scr()
        ACT(z[:, :N], bk[:, :N], AF.Copy, [rb], [rz])
        sq, rs = scrb()
        ACT(sq[:, :N], bk[:, :N], AF.Square, [rb], [rs])
        b2, rb2 = bank()
        MM(b2[:, :N], bones_bf[:, :], sq[:, :N], True, True, [rs, r_bones], [rb2])
        sd, rsd = scr()
        ACT(sd[:, :N], b2[:, :N], AF.Sqrt, [rb2, res("small")], [rsd], scale=1.0 / 64, bias=small[:, 8:9])
        RECIP(sd[:, :N], sd[:, :N], [rsd], [rsd])
        qn, rqn = scr()
        STT("dve", qn[:, :N], z[:, :N], vecs[:, l * LV + gcolidx:l * LV + gcolidx + 1], sd[:, :N], ALU.mult, ALU.mult,
            [rz, rsd, r_vecs], [rqn])
        b3, rb3 = bank()
        MM(b3[:, :N], Rm, qn[:, :N], True, True, [rqn, r_consts], [rb3])
        t1, rt1 = scr()
        TT("dve", t1[:, :N], qn[:, :N], cosT[:, c0:c1], ALU.mult, [rqn, res("rope")], [rt1])
        t2, rt2 = scr()
        TT("dve", t2[:, :N], b3[:, :N], sinT[:, c0:c1], ALU.mult, [rb3, res("rope")], [rt2])
        dst_fn(t1, t2, rt1, rt2)

    def mixer(l, t):
        tile0 = (t == 0)
        last = (t == NTILES - 1) and do_last
        LU = last and os.environ.get('LU', '1') == '1'
        LK = last and os.environ.get('LK', '1') == '1'
        LV_ = last and os.environ.get('LVV', '1') == '1'
        STOP = mix_stop
        c0 = 128 * l if tile0 else 0
        N = NT - c0
        ch0 = c0 // 64
        b = l * LV
        wi = wview(w_in, l)
        r_hT = res("hT"); r_xT = res("xT")
        COPY("act", hT[:, :, 0:128], hhist[l][:, :, :], [res(f"hhist{l}")], [r_hT])
        rmsnorm(l, V_G1, c0, tile0)
        dump("h0", hT[:, 0, 128:], [r_hT], t, l)
        r_u = rl("u", CCH)
        for r_ in r_u:
            pass
        S.add("act", lambda e: e.activation(out=uT[:, :, 0:30], in_=uhist[l][:, :, :], func=AF.Copy), [res(f"uhist{l}")], all_phase())
        for m in range(CCH):
            sl, rs = wload([(lambda s: w3(s, KD, 128), wi[:, :, O_AL + m * 128:O_AL + (m + 1) * 128]),
                            (lambda s: w3(s, KD, 128, 2048), wi[:, :, O_AG + m * 128:O_AG + (m + 1) * 128])])
            wa = w3(sl, KD, 128); wg = w3(sl, KD, 128, 2048)
            bA, rA = bank(); bG, rG = bank()
            for k in range(KD):
                MM(bA[:, :N], wa[:, k, :], hT[:, k, 128 + c0:], k == 0, k == KD - 1, [rs, r_hT], [rA])
            for k in range(KD):
                MM(bG[:, :N], wg[:, k, :], hT[:, k, 128 + c0:], k == 0, k == KD - 1, [rs, r_hT], [rG])
            sg, rsg = scr()
            ACT(sg[:, :N], bG[:, :N], AF.Sigmoid, [rG], [rsg])
            TT("dve", uT[:, m, 30 + c0:], bA[:, :N], sg[:, :N], ALU.mult, [rA, rsg], [r_u[m]])
        dump("u0", uT[:, 0, 30:], [res("u0")], t, l)
        S.add("act", lambda e: e.activation(out=uhist[l][:, :, :], in_=uT[:, :, NT:NT + 30], func=AF.Copy), r_u, [res(f"uhist{l}")])
        if LU:
            for hlf in range(2):
                bk, rb = bank()
                for m4 in range(4):
                    m = hlf * 4 + m4
                    TR(bk[0:32, m4 * 128:(m4 + 1) * 128], uT[:, m, NT - 2:NT + 30], ident, [r_u[m], r_consts], [rb])
                COPY("act", ostage[0:32, hlf * 512:(hlf + 1) * 512], bk[0:32, :], [rb], [res("oT")])
            DMA("sp", convp_d[l, :, :], ostage[2:32, :], semc("ost"), [res("oT")], [res("o_convp")])
        r_c = rl("c", CCH)
        for m in range(CCH):
            wcol = b + V_WDW + m * CW
            TS("dve", cT[:, m, c0:], uT[:, m, c0:c0 + N], vecs[:, wcol:wcol + 1], vecs[:, b + V_BDW + m:b + V_BDW + m + 1],
               ALU.mult, ALU.add, [r_u[m], r_vecs], [r_c[m]])
        for j in range(1, CW):
            for m in range(CCH):
                wcol = b + V_WDW + m * CW
                STT("dve", cT[:, m, c0:], uT[:, m, c0 + j:c0 + j + N], vecs[:, wcol + j:wcol + j + 1], cT[:, m, c0:],
                    ALU.mult, ALU.add, [r_u[m], r_c[m], r_vecs], [r_c[m]])
        dump("c0", cT[:, 0, :], [res("c0")], t, l)
        b1, rb1 = bank(); b2, rb2 = bank()
        for m in range(CCH):
            MM(b1[:, :N], ones_f, cT[:, m, c0:], m == 0, m == CCH - 1, [r_c[m], r_consts], [rb1])
        for m in range(CCH):
            sq, rsq = scr()
            ACT(sq[:, :N], cT[:, m, c0:], AF.Square, [r_c[m]], [rsq])
            MM(b2[:, :N], ones_f, sq[:, :N], m == 0, m == CCH - 1, [rsq, r_consts], [rb2])
        mean, rmean = scr()
        TS("dve", mean[:, :N], b1[:, :N], 1.0 / CD, None, ALU.mult, None, [rb1], [rmean])
        msq, rmsq = scr()
        TT("dve", msq[:, :N], mean[:, :N], mean[:, :N], ALU.mult, [rmean], [rmsq])
        var, rvar = scr()
        STT("dve", var[:, :N], b2[:, :N], 1.0 / CD, msq[:, :N], ALU.mult, ALU.subtract, [rb2, rmsq], [rvar])
        ACT(var[:, :N], var[:, :N], AF.Sqrt, [rvar, res("small")], [rvar], bias=small[:, 8:9])
        RECIP(var[:, :N], var[:, :N], [rvar], [rvar])
        STT("dve", msq[:, :N], mean[:, :N], -1.0, var[:, :N], ALU.mult, ALU.mult, [rmean, rvar], [rmsq])
        r_cn = res("cn")
        held = {id(mean), id(msq), id(var)}
        free_sf = [i for i in range(NSF) if id(sf[i]) not in held]
        for m in range(CCH):
            fi = free_sf[m % len(free_sf)]
            tt_, rtt = sf[fi], res(f"sf{fi}")
            TT("dve", tt_[:, :N], cT[:, m, c0:], var[:, :N], ALU.mult, [r_c[m], rvar], [rtt])
            TT("dve", tt_[:, :N], tt_[:, :N], msq[:, :N], ALU.add, [rtt, rmsq], [rtt])
            ACT(cn[:, m, c0:], tt_[:, :N], AF.Silu, [rtt, r_vecs], [r_cn],
                scale=vecs[:, b + V_LNG + m:b + V_LNG + m + 1], bias=vecs[:, b + V_LNB + m:b + V_LNB + m + 1])
        dump("cn0", cn[:, 0, :], [r_cn], t, l)
        r_qbd = res("qbd")
        S.add("dve", lambda e: e.memset(qbd[:, :, :, :, :], 0.0), (), [r_qbd] + r_c)
        for qp in range(4):
            sl, rs = wload([(lambda s: w3(s, KD, 256), wi[:, :, O_Q + qp * 256:O_Q + (qp + 1) * 256])])
            wq = w3(sl, KD, 256)
            for h2 in range(2):
                qc = qp * 2 + h2
                bk, rb = bank()
                for k in range(KD):
                    MM(bk[:, :N], wq[:, k, h2 * 128:(h2 + 1) * 128], hT[:, k, 128 + c0:], k == 0, k == KD - 1, [rs, r_hT], [rb])

                def dst(t1, t2, rt1, rt2, qc=qc):
                    TT("dve", qbd[0:64, ch0:, qc, 0, :], t1[0:64, :N].rearrange("p (c t) -> p c t", t=64),
                       t2[0:64, :N].rearrange("p (c t) -> p c t", t=64), ALU.add, [rt1, rt2], [r_qbd])
                    TT("dve", qbd[64:128, ch0:, qc, 1, :], t1[64:128, :N].rearrange("p (c t) -> p c t", t=64),
                       t2[64:128, :N].rearrange("p (c t) -> p c t", t=64), ALU.add, [rt1, rt2], [r_qbd])
                qk_chain(l, bk, rb, c0, V_GQ, False, dst, last)
        dump("q0", qbd[:, :, 0, 0, :], [r_qbd], t, l)
        r_kd = res("kd")
        COPY("act", kd[:, :, 0:128], khist[l][:, :, :], [res(f"khist{l}")], [r_kd])
        wk = wi[:, :, O_K:O_K + 256]
        for jp in range(2):
            parts = []
            for jj in range(2):
                j = jp * 2 + jj
                for dup in range(2):
                    parts.append((lambda s, o=(jj * 2 + dup) * 64: w3(s, KD, 256)[:, :, o:o + 64], wk[:, :, j * 64:(j + 1) * 64]))
            sl, rs = wload(parts)
            wkk = w3(sl, KD, 256)
            for jj in range(2):
                j = jp * 2 + jj
                bk, rb = bank()
                for k in range(KD):
                    MM(bk[:, :N], wkk[:, k, jj * 128:(jj + 1) * 128], hT[:, k, 128 + c0:], k == 0, k == KD - 1, [rs, r_hT], [rb])

                def dstk(t1, t2, rt1, rt2, j=j):
                    TT("dve", kd[:, j, 128 + c0:], t1[:, :N], t2[:, :N], ALU.add, [rt1, rt2], [r_kd])
                    if LK:
                        TT("dve", kf[:, j, :], t1[:, NT - 128:NT], t2[:, NT - 128:NT], ALU.add, [rt1, rt2], [res("kf")])
                qk_chain(l, bk, rb, c0, V_GK, True, dstk, last, j)
        S.add("act", lambda e: e.activation(out=khist[l][:, :, :], in_=kd[:, :, NT:NT + 128], func=AF.Copy), [r_kd], [res(f"khist{l}")])
        if LK:
            for j in range(4):
                bk, rb = bank()
                TR(bk[:, 0:128], kf[:, j, :], ident, [res("kf"), r_consts], [rb])
                COPY("act", ostage[:, j * 64:(j + 1) * 64], bk[:, 0:64], [rb], [res("oT")])
            DMA("sp", kp_d[l, :, :], ostage[:, 0:256], semc("ost"), [res("oT")], [res("o_kp")])
        dump("k0", kd[:, 0, 128:], [r_kd], t, l)
        sl, rs = wload([(lambda s: w3(s, KD, 256), wi[:, :, O_V:O_V + 256])])
        wv = w3(sl, KD, 256)
        r_V = res("V")
        first = True
        for blk in range(5):
            for ab in range(2):
                tok0 = blk * 128 + (64 if ab else 0)
                M = 128 if tok0 + 128 <= 128 + NT else 64
                bk, rb = bank()
                for k in range(KD):
                    MM(bk[0:M, 0:256], hT[:, k, tok0:tok0 + M], wv[:, k, :], k == 0, k == KD - 1, [rs, r_hT], [rb])
                Vt = Vb if ab else Va
                src = bk[0:M, 0:256].rearrange("p (j d) -> p j d", d=64)
                wr = [r_V]
                first = False
                COPY("act", Vt[0:M, blk, :, 0:64], src, [rb], wr)
                COPY("dve", Vt[0:M, blk, :, 64:128], src, [rb], [r_V])
        dump("Va1", Va[:, 1, :, :], [r_V], t, l)
        if LV_:
            bk, rb = bank()
            for k in range(KD):
                MM(bk[:, 0:256], hT[:, k, NT:NT + 128], wv[:, k, :], k == 0, k == KD - 1, [rs, r_hT], [rb])
            COPY("dve", ostage[:, 512:768], bk[:, 0:256], [rb], [res("oT")])
            DMA("sp", vp_d[l, :, :], ostage[:, 512:768], semc("ost"), [res("oT")], [res("o_vp")])
        COPY("act", hhist[l][:, :, :], hT[:, :, NT:NT + 128], [r_hT], [res(f"hhist{l}")])
        r_oT = res("oT")
        rsm = res("small")
        for c in range(ch0, 8):
            for j in range(4):
                bS, rS = bank()
                rhs = qbd[:, c, 2 * j:2 * j + 2, :, :].rearrange("p a b t -> p (a b t)")
                MM(bS[:, 0:256], kd[:, j, 64 * c:64 * c + 128], rhs, True, True, [r_kd, r_qbd], [rS])
                MM(bS[0:64, 256:512], kd[:, j, 128 + 64 * c:192 + 64 * c], rhs, True, True, [r_kd, r_qbd], [rS])
                if tile0:
                    bp = bc = small[:, 1:2]
                elif t == 1 and c == 0:
                    bp = small[:, 1:2]; bc = small[:, 0:1]
                elif t == 1 and c == 1:
                    bp = small[:, 2:3]; bc = small[:, 0:1]
                else:
                    bp = bc = small[:, 0:1]
                et, ret = scrb()
                ACT(et[:, 0:256], bS[:, 0:256], AF.Exp, [rS, rsm], [ret], scale=0.125, bias=bp)
                ACT(et[0:64, 256:512], bS[0:64, 256:512], AF.Exp, [rS, rsm], [ret], scale=0.125, bias=bc[0:64, :])
                if c % 2 == 0:
                    Vp = Va[:, c // 2, j, :]; Vc = Va[0:64, c // 2 + 1, j, :]
                else:
                    Vp = Vb[:, (c - 1) // 2, j, :]; Vc = Vb[0:64, (c + 1) // 2, j, :]
                bO, rO = bank()
                MM(bO[:, 0:256], Vp, et[:, 0:256], True, False, [r_V, ret], [rO])
                MM(bO[:, 0:256], Vc, et[0:64, 256:512], False, True, [r_V, ret], [rO])
                MM(bO[:, 256:512], ones_bf[:, :], et[:, 0:256], True, False, [r_ones, ret], [rO])
                MM(bO[:, 256:512], ones_bf[0:64, :], et[0:64, 256:512], False, True, [r_ones, ret], [rO])
                den, rden = scr()
                TT("dve", den[:, 0:256].rearrange("p (g t) -> p g t", t=64), bO[:, 256:512].rearrange("p (g t) -> p g t", t=64),
                   small[:, 16 + 4 * j:20 + 4 * j].unsqueeze(2).to_broadcast([128, 4, 64]), ALU.add, [rO, rsm], [rden])
                RECIP(den[:, 0:256], den[:, 0:256], [rden], [rden])
                for hf in range(2):
                    p0 = hf * 64
                    o_in = bO[p0:p0 + 64, 0:256].rearrange("p (a g t) -> p a g t", g=2, t=64)[:, :, hf, :]
                    d_in = den[p0:p0 + 64, 0:256].rearrange("p (a g t) -> p a g t", g=2, t=64)[:, :, hf, :]
                    TT("dve", oT[p0:p0 + 64, 2 * j:2 * j + 2, 64 * c:64 * c + 64], o_in, d_in, ALU.mult, [rO, rden], [r_oT])
        dump("o0", oT[:, 0, :], [r_oT], t, l)
        dump("o5", oT[:, 5, :], [r_oT], t, l)
        dump("cn5", cn[:, 5, :], [r_cn], t, l)
        merge_out(l, t, c0, NT)

    def merge_out(l, t, c0, c1):
        N = c1 - c0
        wi = wview(w_in, l)
        r_hT = res("hT"); r_xT = res("xT"); r_cn = res("cn"); r_oT = res("oT")
        r_u = rl("u", CCH)
        r_mix = res("mix")
        wco = wview(w_co, l); wao = wview(w_ao, l)
        firstmix = True
        for m in range(KD):
            slA, rsA = wload([(lambda s: w3(s, KD, 128), wi[:, :, O_GC + m * 128:O_GC + (m + 1) * 128]),
                              (lambda s: w3(s, KD, 128, 2048), wi[:, :, O_GA + m * 128:O_GA + (m + 1) * 128])])
            slB, rsB = wload([(lambda s: w3(s, CCH, 128), wco[:, :, m * 128:(m + 1) * 128]),
                              (lambda s: w3(s, CCH, 128, 1024), wao[:, :, m * 128:(m + 1) * 128])])
            wgc = w3(slA, KD, 128); wga = w3(slA, KD, 128, 2048)
            wc_ = w3(slB, CCH, 128); wa_ = w3(slB, CCH, 128, 1024)
            b1, rb1 = bank(); b2, rb2 = bank(); b3, rb3 = bank(); b4, rb4 = bank()
            for k in range(KD):
                MM(b1[:, :N], wgc[:, k, :], hT[:, k, 128 + c0:128 + c1], k == 0, k == KD - 1, [rsA, r_hT], [rb1])
            for k in range(CCH):
                MM(b2[:, :N], wc_[:, k, :], cn[:, k, c0:c1], k == 0, k == CCH - 1, [rsB, r_cn], [rb2])
            for k in range(KD):
                MM(b3[:, :N], wga[:, k, :], hT[:, k, 128 + c0:128 + c1], k == 0, k == KD - 1, [rsA, r_hT], [rb3])
            for k in range(CCH):
                MM(b4[:, :N], wa_[:, k, :], oT[:, k, c0:c1], k == 0, k == CCH - 1, [rsB, r_oT], [rb4])
            s1, rs1 = scr(); s2, rs2 = scr()
            ACT(s1[:, :N], b1[:, :N], AF.Sigmoid, [rb1], [rs1])
            ACT(s2[:, :N], b3[:, :N], AF.Sigmoid, [rb3], [rs2])
            if m == 0:
                dump("sgc0", s1[:, :N], [rs1], t, l)
                dump("sga0", s2[:, :N], [rs2], t, l)
            TT("dve", s1[:, :N], b2[:, :N], s1[:, :N], ALU.mult, [rb2, rs1], [rs1])
            TT("dve", s2[:, :N], b4[:, :N], s2[:, :N], ALU.mult, [rb4, rs2], [rs2])
            if m == 0:
                dump("m1", s1[:, :N], [rs1], t, l)
                dump("m2", s2[:, :N], [rs2], t, l)
            wr = [r_mix] + r_u if firstmix else [r_mix]
            firstmix = False
            TT("dve", mix[:, m, c0:c1], s1[:, :N], s2[:, :N], ALU.add, [rs1, rs2], wr)
        dump("mix0", mix[:, 0, :], [r_mix], t, l)
        wo = wview(w_o, l)
        for mp in range(KD // 2):
            sl, rs = wload([(lambda s: w3(s, KD, 256), wo[:, :, mp * 256:(mp + 1) * 256])])
            ww = w3(sl, KD, 256)
            for h2 in range(2):
                m = mp * 2 + h2
                bk, rb = bank()
                for k in range(KD):
                    MM(bk[:, :N], ww[:, k, h2 * 128:(h2 + 1) * 128], mix[:, k, c0:c1], k == 0, k == KD - 1, [rs, r_mix], [rb])
                TT("dve", xT[:, m, c0:c1], xT[:, m, c0:c1], bk[:, :N], ALU.add, [r_xT, rb], [r_xT])

    def ffn(l, t, c1=NT):
        dump("x1_0", xT[:, 0, :], [res("xT")], t, l)
        tile0 = (t == 0)
        c0 = 128 * l if tile0 else 0
        N = c1 - c0
        r_hT = res("hT"); r_xT = res("xT")
        rmsnorm(l, V_G2, c0, False, c1)
        wg = wview(w_gu, l)
        r_act = res("act")
        firsta = True
        for f in range(FCH):
            sl, rs = wload([(lambda s: w3(s, KD, 128), wg[:, :, f * 128:(f + 1) * 128]),
                            (lambda s: w3(s, KD, 128, 2048), wg[:, :, DFF + f * 128:DFF + (f + 1) * 128])])
            wgg = w3(sl, KD, 128); wuu = w3(sl, KD, 128, 2048)
            bG, rG = bank(); bU, rU = bank()
            for k in range(KD):
                MM(bG[:, :N], wgg[:, k, :], hT[:, k, 128 + c0:128 + c1], k == 0, k == KD - 1, [rs, r_hT], [rG])
            for k in range(KD):
                MM(bU[:, :N], wuu[:, k, :], hT[:, k, 128 + c0:128 + c1], k == 0, k == KD - 1, [rs, r_hT], [rU])
            sg, rsg = scr()
            ACT(sg[:, :N], bG[:, :N], AF.Silu, [rG], [rsg])
            wr = ([r_act, res("mix"), res("qbd"), res("cn"), res("oT"), res("kd"), res("V")] + rl("u", CCH) + rl("c", CCH)) if firsta else [r_act]
            firsta = False
            TT("dve", actT[:, f, c0:c1], bU[:, :N], sg[:, :N], ALU.mult, [rU, rsg], wr)
        dump("act0", actT[:, 0, :], [r_act], t, l)
        wd = wview(w_dn, l)
        for m in range(KD):
            sl0, rs0 = wload([(lambda s: w3(s, 22, 128), wd[:, 0:22, m * 128:(m + 1) * 128])])
            sl1, rs1 = wload([(lambda s: w3(s, 22, 128), wd[:, 22:44, m * 128:(m + 1) * 128])])
            w0 = w3(sl0, 22, 128); w1 = w3(sl1, 22, 128)
            bk, rb = bank()
            for k in range(FCH):
                ww = w0[:, k, :] if k < 22 else w1[:, k - 22, :]
                MM(bk[:, :N], ww, actT[:, k, c0:c1], k == 0, k == FCH - 1, [rs0 if k < 22 else rs1, r_act], [rb])
            TT("dve", xT[:, m, c0:c1], xT[:, m, c0:c1], bk[:, :N], ALU.add, [r_xT, rb], [r_xT])

    def load_tile(t):
        r_xT = res("xT")
        DMA("sp", cosT[:, :], ropec[:, t * NT:(t + 1) * NT], semc("rope"), (), [res("rope")])
        DMA("sp", sinT[:, :], ropes[:, t * NT:(t + 1) * NT], semc("rope"), (), [res("rope")])
        for blk in range(4):
            xs = xstage[blk % 2]; rxs = res(f"xstage{blk % 2}")
            DMA("sp", xs[:, :], xin[t * NT + blk * 128:t * NT + (blk + 1) * 128, :], semc(f"xst{blk % 2}"), (),
                all_phase() if blk < 2 else [rxs])
            for g in range(4):
                bk, rb = bank()
                for k4 in range(4):
                    k = g * 4 + k4
                    TR(bk[:, k4 * 128:(k4 + 1) * 128], xs[:, k * 128:(k + 1) * 128], ident, [rxs, r_consts], [rb])
                COPY("act" if g % 2 == 0 else "dve", xT[:, g * 4:(g + 1) * 4, blk * 128:(blk + 1) * 128],
                     bk[:, :].rearrange("p (k t) -> p k t", t=128), [rb], [r_xT])

    def store_tile(t):
        r_xT = res("xT")
        for blk in range(4):
            xs = xstage[blk % 2]; rxs = res(f"xstage{blk % 2}")
            for g in range(4):
                bk, rb = bank()
                for k4 in range(4):
                    k = g * 4 + k4
                    TR(bk[:, k4 * 128:(k4 + 1) * 128], xT[:, k, blk * 128:(blk + 1) * 128], ident, [r_xT, r_consts], [rb])
                wr = all_phase() if (blk < 2 and g == 0) else [rxs]
                COPY("act" if g % 2 == 0 else "dve", xs[:, g * 512:(g + 1) * 512], bk[:, :], [rb], wr)
            row0 = (t - 1) * NT + blk * 128
            DMA("sp", y_d[row0:row0 + 128, :], xs[:, :], semc(f"yst{blk % 2}"), [rxs], [res(f"o_y{t}_{blk}")])
            out_res.append(res(f"o_y{t}_{blk}"))


    def load_sample(from_y=False):
        r_xT = res("xT")
        c0s = NTILES * NT
        DMA("sp", cosT[:, 0:NS], ropec[:, c0s:c0s + NS], semc("rope"), (), [res("rope")])
        DMA("sp", sinT[:, 0:NS], ropes[:, c0s:c0s + NS], semc("rope"), (), [res("rope")])
        xs = xstage[0]; rxs = res("xstage0")
        if from_y:
            DMA("sp", xs[0:NS, :], y_d[SEQ_PER:SEQ_PER + NS, :], semc("xst0"), [res("o_ys")], all_phase())
        else:
            DMA("sp", xs[0:NS, :], xin[c0s:c0s + NS, :], semc("xst0"), (), all_phase())
        for g in range(4):
            bk, rb = bank()
            for k4 in range(4):
                k = g * 4 + k4
                TR(bk[:, k4 * NS:(k4 + 1) * NS], xs[0:NS, k * 128:(k + 1) * 128], ident[0:NS, 0:NS], [rxs, r_consts], [rb])
            COPY("act" if g % 2 == 0 else "dve", xT[:, g * 4:(g + 1) * 4, 0:NS],
                 bk[:, 0:4 * NS].rearrange("p (k t) -> p k t", t=NS), [rb], [r_xT])

    def store_sample():
        r_xT = res("xT")
        xs = xstage[0]; rxs = res("xstage0")
        for g in range(4):
            bk, rb = bank()
            for k4 in range(4):
                k = g * 4 + k4
                TR(bk[0:NS, k4 * 128:(k4 + 1) * 128], xT[:, k, 0:NS], ident, [r_xT, r_consts], [rb])
            wr = all_phase() if g == 0 else [rxs]
            COPY("act" if g % 2 == 0 else "dve", xs[0:NS, g * 512:(g + 1) * 512], bk[0:NS, :], [rb], wr)
        DMA("sp", y_d[SEQ_PER:SEQ_PER + NS, :], xs[0:NS, :], semc("yst0"), [rxs], [res("o_ys")])
        if res("o_ys") not in out_res:
            out_res.append(res("o_ys"))

    def mixer_sample(l):
        t = -1
        N = NS
        b_ = l * LV
        wi = wview(w_in, l)
        r_hT = res("hT"); r_xT = res("xT"); r_oT = res("oT"); r_cn = res("cn"); r_kd = res("kd"); r_V = res("V")
        r_u = rl("u", CCH); r_c = rl("c", CCH)
        r_un = res("unew")
        uS = [uT[:, m, 0:BPC * 46].rearrange("p (b t) -> p b t", t=46) for m in range(CCH)]
        kS = kd[:, :, :].rearrange("p a b -> p (a b)")[:, 0:BPC * 4 * 144].rearrange("p (b j t) -> p b j t", b=BPC, j=4)
        qS = qbd[:, 0, :, :, :].rearrange("p a b t -> p (a b t)").rearrange("p (b q g t) -> p b q g t", b=BPC, q=8, g=2)
        SST = int(os.environ.get("SST", "99")) if l >= 1 else 99
        r_ckr = res("ckr"); r_ckd = [res("ckd0"), res("ckd1")]
        if SST <= 1:
            return
        DMA("sp", ostage[0:BPC * 30, :], st_conv[l, :, :, :].rearrange("b t c -> (b t) c"), semc("ost"), (), all_phase() + [r_ckr] + r_ckd)
        npad = 0
        if l >= 1:
            for en_ in ("pe", "act", "dve"):
                for _ in range(npad):
                    S.add(en_, lambda e: e.nop(), (), ())
        while bank_i[0] % 8 != int(os.environ.get("SABANK", "4")):
            bank_i[0] += 1
        for hlf in range(2):
            bk, rb = bank()
            for m4 in range(4):
                m = hlf * 4 + m4
                MM(bk[:, m4 * 120:(m4 + 1) * 120], ostage[0:120, m * 128:(m + 1) * 128], ident[0:120, 0:120], True, True, [r_oT, r_consts], [rb])
            dstA = uT[:, hlf * 4:(hlf + 1) * 4, 0:BPC * 46].rearrange("p m (b t) -> p m b t", t=46)[:, :, :, 0:30]
            srcA = bk[:, 0:480].rearrange("p (m b t) -> p m b t", m=4, t=30)
            COPY("dve", dstA, srcA, [rb], [r_u[hlf * 4 + q_] for q_ in range(4)])
        if l >= 1 and os.environ.get("E2", "0") == "1":
            return
        S.add("sp", lambda e: [e.dma_start(out=convs_d[l, bb, 0:14, :], in_=ostage[bb * 30 + 16:bb * 30 + 30, :]) for bb in range(BPC)],
              [r_oT], [res(f"o_pc{l}")], dma=semc("ost"), ndma=BPC)
        out_res.append(res(f"o_pc{l}"))
        if SST <= 2:
            return
        S.add("sp", lambda e: [e.dma_start(out=ckr[:, bb, :], in_=c_k[l, bb, :, :]) for bb in range(BPC)],
              (), [r_ckr, r_cn], dma=semc("ck"), ndma=BPC)
        S.add("sp", lambda e: [e.dma_start(out=ks_d[l, bb, 0:WIN - DEC_T, :], in_=ckr[DEC_T:WIN, bb, :]) for bb in range(BPC)],
              [r_ckr], [res(f"o_pk{l}")], dma=semc("ck"), ndma=BPC)
        out_res.append(res(f"o_pk{l}"))
        for bb in range(BPC):
            cd_ = ckd[bb % 2]; rcd = r_ckd[bb % 2]
            srck = ckr[:, bb, :].rearrange("p (j d) -> p j d", d=64)
            COPY("dve", cd_[:, :, 0, :], srck, [r_ckr], [rcd])
            COPY("act", cd_[:, :, 1, :], srck, [r_ckr], [rcd])
            for j in range(4):
                bk, rb = bank()
                MM(bk[:, 0:128], cd_[:, j, :, :].rearrange("p a d -> p (a d)"), ident, True, True, [rcd, r_consts], [rb])
                COPY("act" if j % 2 == 0 else "dve", kS[:, bb, j, 0:128], bk[:, 0:128], [rb], [r_kd])
        if SST <= 3:
            return
        S.add("sp", lambda e: [e.dma_start(out=ostage[:, bb * 256:(bb + 1) * 256], in_=c_v[l, bb, :, :]) for bb in range(BPC)],
              (), [r_oT], dma=semc("ost"), ndma=BPC)
        S.add("sp", lambda e: [e.dma_start(out=vs_d[l, bb, 0:WIN - DEC_T, :], in_=ostage[DEC_T:WIN, bb * 256:(bb + 1) * 256]) for bb in range(BPC)],
              [r_oT], [res(f"o_pv{l}")], dma=semc("ost"), ndma=BPC)
        out_res.append(res(f"o_pv{l}"))
        srcv = ostage[:, :].rearrange("p (b j d) -> p b j d", b=BPC, d=64)
        COPY("act", Va[:, 0:BPC, :, 0:64], srcv, [r_oT], [r_V])
        COPY("dve", Va[:, 0:BPC, :, 64:128], srcv, [r_oT], [r_V])
        if SST <= 4:
            return
        rmsnorm(l, V_G1, 0, False, NS)
        for m in range(CCH):
            sl, rs = wload([(lambda s: w3(s, KD, 128), wi[:, :, O_AL + m * 128:O_AL + (m + 1) * 128]),
                            (lambda s: w3(s, KD, 128, 2048), wi[:, :, O_AG + m * 128:O_AG + (m + 1) * 128])])
            wa = w3(sl, KD, 128); wg = w3(sl, KD, 128, 2048)
            bA, rA = bank(); bG, rG = bank()
            for k in range(KD):
                MM(bA[:, :N], wa[:, k, :], hT[:, k, 128:128 + N], k == 0, k == KD - 1, [rs, r_hT], [rA])
            for k in range(KD):
                MM(bG[:, :N], wg[:, k, :], hT[:, k, 128:128 + N], k == 0, k == KD - 1, [rs, r_hT], [rG])
            sg, rsg = scr()
            ACT(sg[:, :N], bG[:, :N], AF.Sigmoid, [rG], [rsg])
            TT("dve", cT[:, m, 64:128], bA[:, :N], sg[:, :N], ALU.mult, [rA, rsg], [r_un, r_c[m]])
            COPY("act", uS[m][:, :, 30:46], cT[:, m, 64:128].rearrange("p (b t) -> p b t", t=DEC_T), [r_un, r_c[m]], [r_u[m]])
        for hlf in range(2):
            bk, rb = bank()
            for m4 in range(4):
                m = hlf * 4 + m4
                TR(bk[0:NS, m4 * 128:(m4 + 1) * 128], cT[:, m, 64:128], ident, [r_un, r_c[m], r_consts], [rb])
            COPY("act", ostage[0:NS, hlf * 512:(hlf + 1) * 512], bk[0:NS, :], [rb], [r_oT])
        for bb in range(BPC):
            DMA("sp", convs_d[l, bb, 14:30, :], ostage[bb * DEC_T:(bb + 1) * DEC_T, :], semc("ost"), [r_oT], [res(f"o_cs{l}_{bb}")])
            out_res.append(res(f"o_cs{l}_{bb}"))
        for m in range(CCH):
            wcol = b_ + V_WDW + m * CW
            cdst = cT[:, m, 0:N].rearrange("p (b t) -> p b t", t=DEC_T)
            TS("dve", cdst, uS[m][:, :, 0:DEC_T], vecs[:, wcol:wcol + 1], vecs[:, b_ + V_BDW + m:b_ + V_BDW + m + 1],
               ALU.mult, ALU.add, [r_u[m], r_vecs], [r_c[m]])
            for j in range(1, CW):
                STT("dve", cdst, uS[m][:, :, j:j + DEC_T], vecs[:, wcol + j:wcol + j + 1], cdst,
                    ALU.mult, ALU.add, [r_u[m], r_c[m], r_vecs], [r_c[m]])
        b1, rb1 = bank(); b2, rb2 = bank()
        for m in range(CCH):
            MM(b1[:, :N], ones_f, cT[:, m, 0:N], m == 0, m == CCH - 1, [r_c[m], r_consts], [rb1])
        for m in range(CCH):
            sq, rsq = scr()
            ACT(sq[:, :N], cT[:, m, 0:N], AF.Square, [r_c[m]], [rsq])
            MM(b2[:, :N], ones_f, sq[:, :N], m == 0, m == CCH - 1, [rsq, r_consts], [rb2])
        mean, rmean = scr()
        TS("dve", mean[:, :N], b1[:, :N], 1.0 / CD, None, ALU.mult, None, [rb1], [rmean])
        msq, rmsq = scr()
        TT("dve", msq[:, :N], mean[:, :N], mean[:, :N], ALU.mult, [rmean], [rmsq])
        var, rvar = scr()
        STT("dve", var[:, :N], b2[:, :N], 1.0 / CD, msq[:, :N], ALU.mult, ALU.subtract, [rb2, rmsq], [rvar])
        ACT(var[:, :N], var[:, :N], AF.Sqrt, [rvar, res("small")], [rvar], bias=small[:, 8:9])
        RECIP(var[:, :N], var[:, :N], [rvar], [rvar])
        STT("dve", msq[:, :N], mean[:, :N], -1.0, var[:, :N], ALU.mult, ALU.mult, [rmean, rvar], [rmsq])
        held = {id(mean), id(msq), id(var)}
        free_sf = [i for i in range(NSF) if id(sf[i]) not in held]
        for m in range(CCH):
            fi = free_sf[m % len(free_sf)]
            tt_, rtt = sf[fi], res(f"sf{fi}")
            TT("dve", tt_[:, :N], cT[:, m, 0:N], var[:, :N], ALU.mult, [r_c[m], rvar], [rtt])
            TT("dve", tt_[:, :N], tt_[:, :N], msq[:, :N], ALU.add, [rtt, rmsq], [rtt])
            ACT(cn[:, m, 0:N], tt_[:, :N], AF.Silu, [rtt, r_vecs], [r_cn] + ([r_ckr] + r_ckd if m == 0 else []),
                scale=vecs[:, b_ + V_LNG + m:b_ + V_LNG + m + 1], bias=vecs[:, b_ + V_LNB + m:b_ + V_LNB + m + 1])
        r_qbd = res("qbd")
        S.add("dve", lambda e: e.memset(qbd[:, 0, :, :, :], 0.0), (), [r_qbd, r_un] + r_c)
        for qp in range(4):
            sl, rs = wload([(lambda s: w3(s, KD, 256), wi[:, :, O_Q + qp * 256:O_Q + (qp + 1) * 256])])
            wq = w3(sl, KD, 256)
            for h2 in range(2):
                qc = qp * 2 + h2
                bk, rb = bank()
                for k in range(KD):
                    MM(bk[:, :N], wq[:, k, h2 * 128:(h2 + 1) * 128], hT[:, k, 128:128 + N], k == 0, k == KD - 1, [rs, r_hT], [rb])

                def dst(t1, t2, rt1, rt2, qc=qc):
                    TT("dve", qS[0:64, :, qc, 0, :], t1[0:64, :N].rearrange("p (b t) -> p b t", t=DEC_T),
                       t2[0:64, :N].rearrange("p (b t) -> p b t", t=DEC_T), ALU.add, [rt1, rt2], [r_qbd])
                    TT("dve", qS[64:128, :, qc, 1, :], t1[64:128, :N].rearrange("p (b t) -> p b t", t=DEC_T),
                       t2[64:128, :N].rearrange("p (b t) -> p b t", t=DEC_T), ALU.add, [rt1, rt2], [r_qbd])
                qk_chain(l, bk, rb, 0, V_GQ, False, dst, False, None, NS)
        wk = wi[:, :, O_K:O_K + 256]
        for jp in range(2):
            parts = []
            for jj in range(2):
                j = jp * 2 + jj
                for dup in range(2):
                    parts.append((lambda s, o=(jj * 2 + dup) * 64: w3(s, KD, 256)[:, :, o:o + 64], wk[:, :, j * 64:(j + 1) * 64]))
            sl, rs = wload(parts)
            wkk = w3(sl, KD, 256)
            for jj in range(2):
                j = jp * 2 + jj
                bk, rb = bank()
                for k in range(KD):
                    MM(bk[:, :N], wkk[:, k, jj * 128:(jj + 1) * 128], hT[:, k, 128:128 + N], k == 0, k == KD - 1, [rs, r_hT], [rb])

                def dstk(t1, t2, rt1, rt2, j=j):
                    TT("dve", kS[:, :, j, 128:144], t1[:, :N].rearrange("p (b t) -> p b t", t=DEC_T),
                       t2[:, :N].rearrange("p (b t) -> p b t", t=DEC_T), ALU.add, [rt1, rt2], [r_kd])
                    TT("dve", kf[:, j, 0:N], t1[:, :N], t2[:, :N], ALU.add, [rt1, rt2], [res("kf")])
                qk_chain(l, bk, rb, 0, V_GK, True, dstk, False, j, NS)
        for j in range(4):
            bk, rb = bank()
            TR(bk[0:NS, 0:128], kf[:, j, 0:NS], ident, [res("kf"), r_consts], [rb])
            COPY("act", ostage[0:NS, j * 64:(j + 1) * 64], bk[0:NS, 0:64], [rb], [r_oT])
        for bb in range(BPC):
            DMA("sp", ks_d[l, bb, WIN - DEC_T:WIN, :], ostage[bb * DEC_T:(bb + 1) * DEC_T, 0:256], semc("ost"), [r_oT], [res(f"o_ksn{l}_{bb}")])
            out_res.append(res(f"o_ksn{l}_{bb}"))
        sl, rs = wload([(lambda s: w3(s, KD, 256), wi[:, :, O_V:O_V + 256])])
        wv = w3(sl, KD, 256)
        for bp in range(2):
            bk, rb = bank()
            for b2_ in range(2):
                bb = bp * 2 + b2_
                for k in range(KD):
                    MM(bk[0:DEC_T, b2_ * 256:(b2_ + 1) * 256], hT[:, k, 128 + bb * DEC_T:128 + (bb + 1) * DEC_T], wv[:, k, :],
                       k == 0, k == KD - 1, [rs, r_hT], [rb])
            src = bk[0:DEC_T, :].rearrange("p (b j d) -> p b j d", b=2, d=64)
            COPY("dve", ostage[0:DEC_T, bp * 512:(bp + 1) * 512], bk[0:DEC_T, :], [rb], [r_oT])
            srcs = ostage[0:DEC_T, bp * 512:(bp + 1) * 512].rearrange("p (b j d) -> p b j d", b=2, d=64)
            COPY("act", Vb[0:DEC_T, bp * 2:bp * 2 + 2, :, 0:64], srcs, [r_oT], [r_V])
            COPY("act", Vb[0:DEC_T, bp * 2:bp * 2 + 2, :, 64:128], srcs, [r_oT], [r_V])
        DMA("sp", vs_d[l, :, WIN - DEC_T:WIN, :].rearrange("b t c -> t b c"),
            ostage[0:DEC_T, :].rearrange("p (b c) -> p b c", c=256), semc("ost"), [r_oT], [res(f"o_vsn{l}")])
        out_res.append(res(f"o_vsn{l}"))
        rsm = res("small")
        for j in range(4):
            bS, rS = bank()
            for bb in range(BPC):
                rhs = qS[:, bb, 2 * j:2 * j + 2, :, :].rearrange("p a g t -> p (a g t)")
                MM(bS[:, bb * 128:bb * 128 + 64], kS[:, bb, j, 0:128], rhs, True, True, [r_kd, r_qbd], [rS])
                MM(bS[0:DEC_T, bb * 128 + 64:bb * 128 + 128], kS[:, bb, j, 128:144], rhs, True, True, [r_kd, r_qbd], [rS])
            et, ret = scrb()
            bS3 = bS[:, :].rearrange("p (b c) -> p b c", c=128)
            et3 = et[:, :].rearrange("p (b c) -> p b c", c=128)
            ACT(et3[:, :, 0:64], bS3[:, :, 0:64], AF.Exp, [rS, rsm], [ret], scale=0.125, bias=small[:, 0:1])
            ACT(et3[0:DEC_T, :, 64:128], bS3[0:DEC_T, :, 64:128], AF.Exp, [rS, rsm], [ret], scale=0.125, bias=small[0:DEC_T, 0:1])
            bO, rO = bank()
            for bb in range(BPC):
                c_ = bb * 128
                MM(bO[:, c_:c_ + 64], Va[:, bb, j, :], et[:, c_:c_ + 64], True, False, [r_V, ret], [rO])
                MM(bO[:, c_:c_ + 64], Vb[0:DEC_T, bb, j, :], et[0:DEC_T, c_ + 64:c_ + 128], False, True, [r_V, ret], [rO])
                MM(bO[:, c_ + 64:c_ + 128], ones_bf[:, :], et[:, c_:c_ + 64], True, False, [r_ones, ret], [rO])
                MM(bO[:, c_ + 64:c_ + 128], ones_bf[0:DEC_T, :], et[0:DEC_T, c_ + 64:c_ + 128], False, True, [r_ones, ret], [rO])
            den, rden = scr()
            bO5 = bO[:, :].rearrange("p (b h c t) -> p b h c t", b=BPC, h=2, c=4)
            den4 = den[:, 0:256].rearrange("p (b c t) -> p b c t", b=BPC, c=4)
            TT("dve", den4, bO5[:, :, 1, :, :],
               small[:, 16 + 4 * j:20 + 4 * j].unsqueeze(1).unsqueeze(3).to_broadcast([128, BPC, 4, DEC_T]), ALU.add, [rO, rsm], [rden])
            RECIP(den[:, 0:256], den[:, 0:256], [rden], [rden])
            for hf in range(2):
                p0 = hf * 64
                o_in = bO[p0:p0 + 64, :].rearrange("p (b h a g t) -> p b h a g t", b=BPC, h=2, a=2, g=2)[:, :, 0, :, hf, :]
                d_in = den[p0:p0 + 64, 0:256].rearrange("p (b a g t) -> p b a g t", b=BPC, a=2, g=2)[:, :, :, hf, :]
                o_out = oT[p0:p0 + 64, 2 * j:2 * j + 2, 0:N].rearrange("p a (b t) -> p b a t", t=DEC_T)
                TT("dve", o_out, o_in, d_in, ALU.mult, [rO, rden], [r_oT])
        merge_out(l, t, 0, NS)

    for t in range(ntiles):
        load_tile(t)
        for l in range(depth):
            layer_setup(l)
            if do_mixer:
                mixer(l, t)
            if do_ffn:
                ffn(l, t)
        if t >= 1:
            store_tile(t)
    if do_sample:
        load_sample()
        for l in range(depth):
            if l >= 1:
                store_sample()
                load_sample(from_y=True)
            layer_setup(l)
            mixer_sample(l)
            ffn(l, -1, NS)
        store_sample()
    if do_mixer and ntiles == NTILES and do_last:
        for n, f in (("o_convp", 'LU'), ("o_kp", 'LK'), ("o_vp", 'LVV')):
            if os.environ.get(f, '1') == '1':
                out_res.append(res(n))
    S.add("sp", lambda e: e.nop() if False else None, out_res, ())
    fin = S.q["sp"].pop()
    final_deps = fin.deps

    S.assign()

    def emit(en):
        def f(e):
            S.emit_one(en, e)
            if en == "sp":
                done = {}
                for d in final_deps:
                    k = id(d.sem)
                    if d.val > done.get(k, (None, 0))[1]:
                        done[k] = (d.sem, d.val)
                for s_, v in done.values():
                    e.wait_ge(s_, v)
        return f
    with nc.Block() as block:
        block.tensor(emit("pe")); block.scalar(emit("act")); block.vector(emit("dve"))
        block.gpsimd(emit("pool")); block.sync(emit("sp"))
    print("ops:", {e: len(S.q[e]) for e in S.ENGS}, "sbuf hi", hi[0])
    nc._dbg_names = dbg_names
    return nc


def _host_tables(core):
    half = 8
    inv_freq = (500000.0 ** (-np.arange(half, dtype=np.float32) * 2.0 / 16)).astype(np.float32)
    pos_p = (np.arange(NTILES * NT, dtype=np.float32) + np.float32(core * SEQ_PER - HALO))
    pos_s = np.tile(np.arange(DEC_T, dtype=np.float32) + np.float32(PAST), BPC)
    pos = np.concatenate([pos_p, pos_s]).astype(np.float32)
    ang = (pos[None, :] * inv_freq[:, None]).astype(np.float32)
    c8 = np.cos(ang).astype(np.float32); s8 = np.sin(ang).astype(np.float32)
    T = pos.shape[0]
    cos64 = np.ones((64, T), np.float32); sin64 = np.zeros((64, T), np.float32)
    cos64[0:8] = c8; cos64[8:16] = c8
    sin64[0:8] = -s8; sin64[8:16] = s8
    return np.concatenate([cos64, cos64], 0), np.concatenate([sin64, sin64], 0)


def _consts():
    c = np.zeros((128, 512), np.float32)
    c[:, 0:128] = np.eye(128, dtype=np.float32)
    Rm = np.zeros((128, 128), np.float32)
    for m in range(128):
        d = m % 64
        if d < 8:
            Rm[m + 8, m] = 1.0
        elif d < 16:
            Rm[m - 8, m] = 1.0
    c[:, 128:256] = Rm
    c[:, 256:384] = 1.0
    c[0:64, 384:448] = 1.0
    c[64:128, 448:512] = 1.0
    return c


def _vecs(inp, core):
    v = np.zeros((128, NV), np.float32)

    def fm(a, n):
        return np.ascontiguousarray(a.reshape(n, 128).T)
    for l in range(DEPTH):
        b = l * LV
        v[:, b + V_G1:b + V_G1 + 16] = fm(inp["norm_mix_g"][l], 16)
        v[:, b + V_G2:b + V_G2 + 16] = fm(inp["norm_ffn_g"][l], 16)
        v[:, b + V_BDW:b + V_BDW + 8] = fm(inp["b_dw"][l], 8)
        v[:, b + V_LNG:b + V_LNG + 8] = fm(inp["conv_ln_g"][l], 8)
        v[:, b + V_LNB:b + V_LNB + 8] = fm(inp["conv_ln_b"][l], 8)
        wd = inp["w_dw"][l]
        for m in range(8):
            v[:, b + V_WDW + m * CW:b + V_WDW + (m + 1) * CW] = wd[:, m * 128:(m + 1) * 128].T
        v[:, b + V_GQ] = np.tile(inp["q_norm_g"][l], 2)
        v[:, b + V_GK] = np.tile(inp["k_norm_g"][l], 2)
        v[:, b + V_GQR:b + V_GQR + 64] = inp["q_norm_g"][l][None, :]
        v[:, b + V_GKR:b + V_GKR + 64] = inp["k_norm_g"][l][None, :]
        v[:, b + V_SNK:b + V_SNK + 16] = inp["sinks"][l][None, :]
    valid = 0.0 if core == 0 else 1.0
    mb = 0.0 if core != 0 else -30000.0
    v[:, V_CM] = valid
    v[:, V_CM + 2] = mb
    v[0:64, V_CM + 3] = mb
    return v


_NC_CACHE = {}


def kernel(**inputs):
    inp = {k: np.asarray(v) for k, v in inputs.items()}
    xp = inp["x_prompt"][0]
    xs = inp["x_sample"]
    if "nc" not in _NC_CACHE:
        _NC_CACHE["nc"] = build_program()
    nc = _NC_CACHE["nc"]
    consts = _consts()
    in_maps = []
    for c in range(NCORES):
        xin = np.zeros((NTILES * NT + NS, D), np.float32)
        lo = c * SEQ_PER - HALO
        if lo < 0:
            xin[HALO:HALO + SEQ_PER] = xp[0:SEQ_PER]
        else:
            xin[0:NTILES * NT] = xp[lo:lo + NTILES * NT]
        xin[NTILES * NT:] = xs[c * BPC:(c + 1) * BPC].reshape(NS, D)
        rc, rs = _host_tables(c)
        in_maps.append({
            "xin": xin, "ropec": rc, "ropes": rs, "vecs": _vecs(inp, c), "consts": consts,
            "w_in": inp["w_in"], "w_conv_out": inp["w_conv_out"], "w_attn_out": inp["w_attn_out"],
            "w_out": inp["w_out"], "w_gate_up": inp["w_gate_up"], "w_down": inp["w_down"],
            "state_conv": np.ascontiguousarray(inp["state_conv"][:, c * BPC:(c + 1) * BPC]),
            "cache_k": np.ascontiguousarray(inp["cache_k"][:, c * BPC:(c + 1) * BPC].reshape(DEPTH, BPC, WIN, 256)),
            "cache_v": np.ascontiguousarray(inp["cache_v"][:, c * BPC:(c + 1) * BPC].reshape(DEPTH, BPC, WIN, 256)),
        })
    res = run_bass_kernel_spmd(nc, in_maps, core_ids=list(range(NCORES)))
    R = res.results
    y_prompt = np.concatenate([R[c]["y"][0:SEQ_PER] for c in range(NCORES)], 0)[None]
    y_sample = np.concatenate([R[c]["y"][SEQ_PER:].reshape(BPC, DEC_T, D) for c in range(NCORES)], 0)
    conv_p = R[NCORES - 1]["convp"][:, None]
    k_p = R[NCORES - 1]["kp"].reshape(DEPTH, 1, WIN, 4, 64)
    v_p = R[NCORES - 1]["vp"].reshape(DEPTH, 1, WIN, 4, 64)
    conv_s = np.concatenate([R[c]["convs"] for c in range(NCORES)], 1)
    k_s = np.concatenate([R[c]["ks"].reshape(DEPTH, BPC, WIN, 4, 64) for c in range(NCORES)], 1)
    v_s = np.concatenate([R[c]["vs"].reshape(DEPTH, BPC, WIN, 4, 64) for c in range(NCORES)], 1)
    return (y_prompt.astype(np.float32), y_sample.astype(np.float32), conv_p.astype(np.float32),
            k_p.astype(np.float32), v_p.astype(np.float32), conv_s.astype(np.float32),
            k_s.astype(np.float32), v_s.astype(np.float32))
```
